# Optimizing a Trainium2 kernel written in Bass

```python
import math
import jax, jax.numpy as jnp
from jax import lax
import numpy as np

D_MODEL = 1024
BATCH = 2
SEQ = 16384
DEPTH = 2

D_FF = 2816
D_MIX = D_MODEL
HEAD_DIM = 64
A_HEADS = 4
A_DIM = A_HEADS * HEAD_DIM
CHUNK = 128
B_GROUPS = 4
B_DIM = B_GROUPS * HEAD_DIM
B_CONV = 3
C_HEADS = 8
C_DIM = C_HEADS * HEAD_DIM
C_CONV = 4
RG_C = 8.0
IN_COLS = 2 * A_DIM + 3 * B_DIM + 2 * C_DIM
ALPHA = (2.0 * DEPTH) ** 0.25
BETA = (8.0 * DEPTH) ** -0.25
LN_EPS = 1e-5

kernel_name = "hybrid_sgu_shortconv_rglru_macaron_deepnorm"


def layer_norm(x, g, b):
    xf = x.astype(jnp.float32)
    mu = jnp.mean(xf, axis=-1, keepdims=True)
    xc = xf - mu
    var = jnp.mean(xc * xc, axis=-1, keepdims=True)
    y = xc * lax.rsqrt(var + LN_EPS) * g.astype(jnp.float32) + b.astype(jnp.float32)
    return y.astype(x.dtype)


def swiglu(x, w_gate, w_up, w_down):
    return (jax.nn.silu(x @ w_gate) * (x @ w_up)) @ w_down


def causal_dwconv(x, w):
    k_width = w.shape[0]
    s = x.shape[1]
    xp = jnp.pad(x, ((0, 0), (k_width - 1, 0), (0, 0)))
    y = xp[:, 0:s] * w[0]
    for k in range(1, k_width):
        y = y + xp[:, k:k + s] * w[k]
    return y


def mixer_spatial_gating(z, ln_g, ln_b, w_s, b_s):
    z = jax.nn.gelu(z)
    u, v = jnp.split(z, 2, axis=-1)
    v = layer_norm(v, ln_g, ln_b)
    bsz, s, _ = v.shape
    vc = v.reshape(bsz, s // CHUNK, CHUNK, A_HEADS, HEAD_DIM)
    mask = jnp.tril(jnp.ones((CHUNK, CHUNK), dtype=bool))
    w = jnp.where(mask[None], w_s, jnp.zeros_like(w_s))
    mixed = jnp.einsum('hts,bnshd->bnthd', w, vc) + b_s.T[None, None, :, :, None]
    return u * mixed.reshape(bsz, s, A_DIM)


def mixer_short_conv(z, conv_w):
    b_gate, c_gate, xin = jnp.split(z, 3, axis=-1)
    return b_gate * causal_dwconv(c_gate * xin, conv_w)


def _lru_combine(left, right):
    a1, b1 = left
    a2, b2 = right
    return a1 * a2, a2 * b1 + b2


def mixer_rglru(z, conv_w, conv_b, w_a, b_a, w_i, b_i, lam):
    gate, xr = jnp.split(z, 2, axis=-1)
    xr = causal_dwconv(xr, conv_w) + conv_b
    bsz, s, _ = xr.shape
    xh = xr.reshape(bsz, s, C_HEADS, HEAD_DIM)
    r = jax.nn.sigmoid(jnp.einsum('bshd,hde->bshe', xh, w_a) + b_a).reshape(bsz, s, C_DIM)
    i = jax.nn.sigmoid(jnp.einsum('bshd,hde->bshe', xh, w_i) + b_i).reshape(bsz, s, C_DIM)
    log_a = -RG_C * r.astype(jnp.float32) * jax.nn.softplus(-lam.astype(jnp.float32))
    a = jnp.exp(log_a)
    mult = jnp.sqrt(jnp.maximum(1.0 - jnp.exp(2.0 * log_a), 0.0))
    bx = mult * (i * xr).astype(jnp.float32)
    _, h = lax.associative_scan(_lru_combine, (a, bx), axis=1)
    return jax.nn.gelu(gate) * h.astype(gate.dtype)


def setup_inputs(seed: int = 0) -> dict:
    key = jax.random.key(seed)
    ks = jax.random.split(key, 24)

    def nrm(k, shape, fan_in, scale=1.0):
        return jax.random.normal(k, shape, jnp.float32) * (scale * fan_in ** -0.5)

    x = jax.random.normal(ks[0], (BATCH, SEQ, D_MODEL), jnp.float32)
    ln_g = 1.0 + 0.02 * jax.random.normal(ks[1], (DEPTH, 3, D_MODEL), jnp.float32)
    ln_b = 0.02 * jax.random.normal(ks[2], (DEPTH, 3, D_MODEL), jnp.float32)
    ffn_w_gate = nrm(ks[3], (DEPTH, 2, D_MODEL, D_FF), D_MODEL)
    ffn_w_up = nrm(ks[4], (DEPTH, 2, D_MODEL, D_FF), D_MODEL)
    ffn_w_down = nrm(ks[5], (DEPTH, 2, D_FF, D_MODEL), D_FF, BETA)
    w_in = nrm(ks[6], (DEPTH, D_MODEL, IN_COLS), D_MODEL)
    sgu_ln_g = 1.0 + 0.02 * jax.random.normal(ks[7], (DEPTH, A_DIM), jnp.float32)
    sgu_ln_b = 0.02 * jax.random.normal(ks[8], (DEPTH, A_DIM), jnp.float32)
    sgu_w = nrm(ks[9], (DEPTH, A_HEADS, CHUNK, CHUNK), CHUNK)
    sgu_b = 1.0 + 0.02 * jax.random.normal(ks[10], (DEPTH, A_HEADS, CHUNK), jnp.float32)
    sconv_w = nrm(ks[11], (DEPTH, B_CONV, B_DIM), B_CONV)
    rg_conv_w = nrm(ks[12], (DEPTH, C_CONV, C_DIM), C_CONV)
    rg_conv_b = 0.02 * jax.random.normal(ks[13], (DEPTH, C_DIM), jnp.float32)
    rg_w_a = nrm(ks[14], (DEPTH, C_HEADS, HEAD_DIM, HEAD_DIM), HEAD_DIM)
    rg_b_a = 0.02 * jax.random.normal(ks[15], (DEPTH, C_HEADS, HEAD_DIM), jnp.float32)
    rg_w_i = nrm(ks[16], (DEPTH, C_HEADS, HEAD_DIM, HEAD_DIM), HEAD_DIM)
    rg_b_i = 0.02 * jax.random.normal(ks[17], (DEPTH, C_HEADS, HEAD_DIM), jnp.float32)
    u = jax.random.uniform(ks[18], (DEPTH, C_DIM), jnp.float32, minval=0.9, maxval=0.999)
    a0 = u ** (1.0 / RG_C)
    rg_lambda = jnp.log(a0) - jnp.log1p(-a0)
    w_out = nrm(ks[19], (DEPTH, D_MIX, D_MODEL), D_MIX, BETA)
    return {"x": x, "ln_g": ln_g, "ln_b": ln_b, "ffn_w_gate": ffn_w_gate, "ffn_w_up": ffn_w_up,
            "ffn_w_down": ffn_w_down, "w_in": w_in, "sgu_ln_g": sgu_ln_g, "sgu_ln_b": sgu_ln_b,
            "sgu_w": sgu_w, "sgu_b": sgu_b, "sconv_w": sconv_w, "rg_conv_w": rg_conv_w,
            "rg_conv_b": rg_conv_b, "rg_w_a": rg_w_a, "rg_b_a": rg_b_a, "rg_w_i": rg_w_i,
            "rg_b_i": rg_b_i, "rg_lambda": rg_lambda, "w_out": w_out}


def reference(x, ln_g, ln_b, ffn_w_gate, ffn_w_up, ffn_w_down, w_in, sgu_ln_g, sgu_ln_b,
              sgu_w, sgu_b, sconv_w, rg_conv_w, rg_conv_b, rg_w_a, rg_b_a, rg_w_i, rg_b_i,
              rg_lambda, w_out):
    split_a = 2 * A_DIM
    split_b = split_a + 3 * B_DIM
    for l in range(DEPTH):
        f = swiglu(x, ffn_w_gate[l, 0], ffn_w_up[l, 0], ffn_w_down[l, 0])
        x = layer_norm(ALPHA * x + 0.5 * f, ln_g[l, 0], ln_b[l, 0])
        z = x @ w_in[l]
        z_a = z[..., :split_a]
        z_b = z[..., split_a:split_b]
        z_c = z[..., split_b:]
        y_a = mixer_spatial_gating(z_a, sgu_ln_g[l], sgu_ln_b[l], sgu_w[l], sgu_b[l])
        y_b = mixer_short_conv(z_b, sconv_w[l])
        y_c = mixer_rglru(z_c, rg_conv_w[l], rg_conv_b[l], rg_w_a[l], rg_b_a[l],
                          rg_w_i[l], rg_b_i[l], rg_lambda[l])
        y = jnp.concatenate([y_a, y_b, y_c], axis=-1) @ w_out[l]
        x = layer_norm(ALPHA * x + y, ln_g[l, 1], ln_b[l, 1])
        f = swiglu(x, ffn_w_gate[l, 1], ffn_w_up[l, 1], ffn_w_down[l, 1])
        x = layer_norm(ALPHA * x + 0.5 * f, ln_g[l, 2], ln_b[l, 2])
    return x
```

```python
import numpy as np
from contextlib import ExitStack
import concourse.bass as bass
import concourse.mybir as mybir
from concourse.bass_utils import run_bass_kernel_spmd

F32 = mybir.dt.float32
BF16 = mybir.dt.bfloat16
AF = mybir.ActivationFunctionType
ALU = mybir.AluOpType
AX = mybir.AxisListType

D = 1024
DFF = 2816
NJ = DFF // 128
NB = 1024
NTT = NB // 128
NSTEP = 4
NLAYER = 2
INC = 2304
ALPHA = (2.0 * NLAYER) ** 0.25
LN_EPS = 1e-5
NPP = 40
RING = 6
GELU = AF.Gelu_apprx_tanh

PE, ACT, DVE, POOL, SP = 0, 1, 2, 3, 4
DQ_RING0 = 5
DQ_WD, DQ_X, DQ_OUT, DQ_MISC, DQ_AGA, DQ_AGB, CC, DQ_LN = 11, 12, 13, 14, 15, 16, 17, 18
DQ_XT0 = 19
DQ_OT0 = 27
DQ_BLK = 35
DQ_SG = 36
NSEM = 37


class Res:
    __slots__ = ("w", "r")

    def __init__(self):
        self.w = {}
        self.r = {}

    def inherit(self, others):
        for o in others:
            for d_, s_ in ((self.w, o.w), (self.r, o.r)):
                for i, c in s_.items():
                    if d_.get(i, 0) < c:
                        d_[i] = c


class Eng:
    def __init__(self, ctx, idx, eng, same):
        self.ctx, self.idx, self.e, self.same = ctx, idx, eng, same
        self.count = 0
        self.seen = {}

    def wait_tok(self, i, c):
        if i == self.idx and not self.same:
            return
        if self.seen.get(i, 0) >= c:
            return
        self.e.wait_ge(self.ctx.sems[i], c)
        self.seen[i] = c

    def deps(self, reads, writes):
        for r in reads:
            for i, c in r.w.items():
                self.wait_tok(i, c)
        for w in writes:
            for i, c in w.w.items():
                self.wait_tok(i, c)
            for i, c in w.r.items():
                self.wait_tok(i, c)

    def done(self, ins, reads, writes):
        self.count += 1
        ins.then_inc(self.ctx.sems[self.idx], 1)
        for r in reads:
            r.r[self.idx] = self.count
        for w in writes:
            w.w = {self.idx: self.count}
            w.r = {}

    def op(self, fn, reads=(), writes=()):
        self.deps(reads, writes)
        self.done(fn(), reads, writes)

    def group(self, fns, reads=(), writes=()):
        self.deps(reads, writes)
        ins = None
        for f in fns:
            ins = f()
        self.done(ins, reads, writes)

    def dma(self, dq, pairs, reads=(), writes=()):
        self.deps(reads, writes)
        ctx = self.ctx
        for out, in_ in pairs:
            self.e.dma_start(out=out, in_=in_).then_inc(ctx.sems[dq], 16)
            ctx.dcount[dq] += 16
        c = ctx.dcount[dq]
        for r in reads:
            r.r[dq] = c
        for w in writes:
            w.w = {dq: c}
            w.r = {}


class Ctx:
    pass


def build(nsteps=NSTEP, nlayers=NLAYER, dbg=None):
    nc = bass.Bass("TRN2", target_bir_lowering=False)
    es = ExitStack()
    ctx = Ctx()
    ctx.dcount = [0] * NSEM

    def din(name, shape):
        return nc.dram_tensor(name, shape, F32, kind="ExternalInput").ap()

    x_d = din("x", [nsteps, NB, D])
    out_d = nc.dram_tensor("out", [nsteps, NB, D], F32, kind="ExternalOutput").ap()
    wg_d = din("ffn_w_gate", [2, 2, D, DFF])
    wu_d = din("ffn_w_up", [2, 2, D, DFF])
    wdn_d = din("ffn_w_down", [2, 2, DFF, D])
    win_d = din("w_in", [2, D, INC])
    wout_d = din("w_out", [2, D, D])
    lng_d = din("ln_g", [2, 3, D])
    lnb_d = din("ln_b", [2, 3, D])
    sgg_d = din("sgu_ln_g", [2, 256])
    sgb_d = din("sgu_ln_b", [2, 256])
    pp_d = din("pp", [2, 128, NPP])
    wsT_d = din("wsT", [2, 128, 4, 128])
    blk_d = din("blk", [2, 128, 2, 4, 128])
    sgub_d = din("sgub", [2, 2, 2, 128])
    ident_d = din("ident", [128, 128])
    triu_d = din("triu", [128, 128])
    ind2_d = din("ind2", [2, 128])
    sel_d = din("sel", [128, 4])
    dbg_d = {}
    if dbg:
        for name, shape in dbg.items():
            dbg_d[name] = nc.dram_tensor("dbg_" + name, shape, F32, kind="ExternalOutput").ap()

    nag = nsteps * nlayers
    ag1i = [nc.dram_tensor(f"ag1i_{i}", [128, 16], F32) for i in range(nag)]
    ag1o = [nc.dram_tensor(f"ag1o_{i}", [512, 16], F32) for i in range(nag)]
    ag2i = [nc.dram_tensor(f"ag2i_{i}", [128, 8], F32) for i in range(nag)]
    ag2o = [nc.dram_tensor(f"ag2o_{i}", [512, 8], F32) for i in range(nag)]

    def sb(name, shape, dt=F32):
        return es.enter_context(nc.sbuf_tensor(name, shape, dt))

    ctx.sems = [es.enter_context(nc.semaphore(f"s{i}")) for i in range(NSEM)]
    pe = Eng(ctx, PE, nc.tensor, False)
    act = Eng(ctx, ACT, nc.scalar, True)
    dve = Eng(ctx, DVE, nc.vector, True)
    pool = Eng(ctx, POOL, nc.gpsimd, True)
    sp = Eng(ctx, SP, nc.sync, False)

    xres = sb("xres", [128, NTT, D])
    xT = sb("xT", [128, 8, NB], BF16)
    big = sb("big", [128, NJ * NB], BF16)
    wdb = sb("wdb", [128, NJ * D], BF16)
    ring_t = [sb(f"ring{i}", [128, 8, 256], BF16) for i in range(RING)]
    tmpA = [sb(f"tmpA{i}", [128, 512]) for i in range(2)]
    g_bc = sb("g_bc", [128, D])
    b_bc = sb("b_bc", [128, D])
    ident = sb("ident_sb", [128, 128])
    ind2 = sb("ind2_sb", [2, 128])
    sel = sb("sel_sb", [128, 4])
    epsc = sb("epsc", [128, 1])
    pbuf = sb("pbuf", [128, 2, 1032])
    st = [sb(f"st{i}", [128, 2, 6]) for i in range(2)]
    mv = [sb(f"mv{i}", [128, 2]) for i in range(2)]
    rs = [sb(f"rs{i}", [128, 1]) for i in range(2)]
    nmr = [sb(f"nmr{i}", [128, 1]) for i in range(2)]
    ex1 = sb("ex1", [128, 16])
    cand1 = sb("cand1", [128, 4, 16])
    hsel = sb("hsel", [128, 16])
    ex2 = sb("ex2", [128, 8])
    cand2 = sb("cand2", [128, 4, 8])
    sumr = sb("sumr", [128, 4, 2])
    srt = sb("srt", [128, 4])
    Sst = sb("Sst", [128, 4, 4])
    hin = sb("hin", [128, 4])
    tmpc = sb("tmpc", [128, 4])
    st4 = [sb(f"st4_{i}", [128, 4, 6]) for i in range(2)]
    mv4 = [sb(f"mv4_{i}", [128, 4, 2]) for i in range(2)]
    rs4 = [sb(f"rs4_{i}", [128, 4]) for i in range(2)]
    nmr4 = [sb(f"nmr4_{i}", [128, 4]) for i in range(2)]
    pp = [sb(f"pp{l}", [128, NPP]) for l in range(nlayers)]
    cp = [sb(f"cp{l}", [128, 4]) for l in range(nlayers)]
    hcp = [sb(f"hcp{l}", [128, 4]) for l in range(nlayers)]
    hb = [sb(f"hb{l}", [128, 8]) for l in range(nlayers)]
    WT = [sb(f"WT{l}", [128, 4, 128], BF16) for l in range(nlayers)]
    blk1 = sb("blk_sb", [128, 2, 4, 128], BF16)
    blk = [blk1 for l in range(nlayers)]
    sgg_bc = sb("sgg", [128, 256])
    sgb_bc = sb("sgb", [128, 256])
    vt4 = [sb(f"vt4_{i}", [128, 4, 256]) for i in range(2)]
    xc0 = sb("xc0", [128, 4, 512])
    sgub1 = sb("sgub_sb", [2, 2, 128])
    sgub = [sgub1 for l in range(nlayers)]
    prev1 = [sb(f"prev1_{l}", [128, 16]) for l in range(nlayers)]
    carry = [sb(f"carry{l}", [128, 4]) for l in range(nlayers)]
    ps = [es.enter_context(nc.psum_tensor(f"ps{i}", [128, 512], F32)) for i in range(8)]

    hT = big[:, :].rearrange("p (j t) -> p j t", j=NJ)
    wd3 = wdb[:, :].rearrange("p (j c) -> p j c", j=NJ)

    def fview(region, boff, n, c=None):
        v = region[:, boff // 2: boff // 2 + 2 * n].bitcast(F32)
        if c:
            v = v.rearrange("p (c t) -> p c t", c=c)
        return v

    a_buf = fview(big, 0, 4096, 4)
    bx_buf = fview(big, 16384, 4096, 4)
    T_buf = fview(big, 32768, 2048, 4)
    xcbf = big[:, 40960 // 2: 40960 // 2 + 2048].rearrange("p (c t) -> p c t", c=4)
    xr_buf = fview(wdb, 0, 4128, 4)
    xc_buf = fview(wdb, 16512, 2048, 4)
    yT = wdb[:, 24704 // 2: 24704 // 2 + 8192].rearrange("p (c t) -> p c t", c=8)
    xc_sb = [xc0[:, :, :], xc_buf]
    qbuf = vt4[0][:, :, :].rearrange("p q d -> p (q d)").rearrange("p (c t) -> p c t", c=2)
    pflat = pbuf[:, :, :].rearrange("p c t -> p (c t)").bitcast(BF16)
    vlnz4 = [pflat[:, i * 2048:(i + 1) * 2048].rearrange("p (q c h d) -> p q c h d", q=4, c=2, h=2) for i in range(2)]
    gg = [fview(big, 32768, 2048, 2), fview(wdb, 16512, 2048, 2)]

    R_xres = [Res() for _ in range(NTT)]
    R_xT = [Res() for _ in range(NTT)]
    R_hT = [[Res(), Res()] for _ in range(NJ)]
    R_wd = Res()
    R_ring = [Res() for _ in range(RING)]
    R_tmp = [Res(), Res()]
    R_lngb = Res()
    R_small = [Res(), Res()]
    R_bank = [Res() for _ in range(8)]
    R_par = Res()
    R_a = [[Res(), Res()] for _ in range(4)]
    R_bx = [[Res(), Res()] for _ in range(4)]
    R_Tc = [Res() for _ in range(4)]
    R_xcbf = Res()
    R_xc = [Res(), Res()]
    R_xcp = [Res(), Res()]
    R_sg = Res()
    R_ident = Res()
    R_blk = Res()
    R_xrs = [Res(), Res()]
    R_xrh = Res()
    R_tmpc = Res()
    R_vt4 = [Res(), Res()]
    R_vl4 = [Res(), Res()]
    R_sm4 = [Res(), Res()]
    R_gg = [Res() for _ in range(4)]
    R_yT = [[Res(), Res()] for _ in range(8)]
    R_p, R_q = Res(), Res()
    R_ex1, R_cand1, R_hsel, R_ex2, R_cand2 = Res(), Res(), Res(), Res(), Res()
    R_sumr, R_chain = Res(), Res()
    R_prev1 = [Res() for _ in range(nlayers)]
    R_carry = [Res() for _ in range(nlayers)]
    R_ag = Res()

    def flat(ll):
        return [r for sub in ll for r in sub]

    big_mixer = flat(R_a) + flat(R_bx) + R_Tc + [R_xcbf]
    wd_mixer = R_xrs + [R_xrh, R_xc[1], R_xcp[1]] + flat(R_yT)

    live = [False] * 8
    nxt = [0]

    def balloc():
        for k in range(8):
            b = (nxt[0] + k) % 8
            if not live[b]:
                live[b] = True
                nxt[0] = (b + 1) % 8
                return b
        raise RuntimeError("no free PSUM bank")

    def bfree(*bs):
        for b in bs:
            live[b] = False

    tmpi = [0]

    def tmp_next():
        tmpi[0] ^= 1
        return tmpi[0]

    loads = []
    for st_ in range(nsteps):
        for l in range(nlayers):
            for k in (0, None, 1):
                if k is None:
                    for c0 in (1792, 2048, 768, 1024, 256, 0, 1280, 1536, 512):
                        loads.append(win_d[l][:, c0:c0 + 256])
                    for c0 in (0, 256, 512, 768):
                        loads.append(wout_d[l][:, c0:c0 + 256])
                else:
                    for s_ in range(11):
                        loads.append(wg_d[l, k][:, s_ * 256:(s_ + 1) * 256])
                        loads.append(wu_d[l, k][:, s_ * 256:(s_ + 1) * 256])
    rstate = {"issued": 0, "consumed": 0, "taken": 0}

    def ring_fill():
        while rstate["issued"] < len(loads) and rstate["issued"] - rstate["consumed"] < RING:
            n = rstate["issued"]
            s_ = n % RING
            src = loads[n].rearrange("(kc p) c -> p kc c", p=128)
            pool.dma(DQ_RING0 + s_, [(ring_t[s_][:, :, :], src)], writes=[R_ring[s_]])
            rstate["issued"] += 1

    def ring_take():
        n = rstate["taken"]
        assert n < rstate["issued"], "ring underflow"
        rstate["taken"] += 1
        return n % RING

    def ring_release(k=1):
        rstate["consumed"] += k
        ring_fill()

    V, A, P = nc.vector, nc.scalar, nc.tensor

    sp.dma(DQ_X, [(ident[:, :], ident_d[:, :])], writes=[R_ident])
    for tt in range(NTT):
        sp.dma(DQ_XT0 + tt, [(xres[:, tt, :], x_d[0, tt * 128:(tt + 1) * 128, :])], writes=[R_xres[tt]])
    sp.dma(DQ_MISC, [(ind2[:, :], ind2_d[:, :]), (sel[:, :], sel_d[:, :]),
                     (tmpA[1][:, 0:128], triu_d[:, :])], writes=[R_par, R_tmp[1]])
    for l in range(nlayers):
        sp.dma(DQ_MISC, [(pp[l][:, :], pp_d[l])], writes=[R_par])
    for l in range(nlayers):
        sp.dma(DQ_MISC, [(tmpA[0][:, :].rearrange("p (h t) -> p h t", h=4), wsT_d[l])], writes=[R_tmp[0]])
        for h in range(4):
            dve.op(lambda: V.tensor_tensor(WT[l][:, h, :], tmpA[0][:, h * 128:(h + 1) * 128], tmpA[1][:, 0:128], ALU.mult),
                   reads=[R_tmp[0], R_tmp[1], R_par], writes=[R_par])
        lam = pp[l][:, 28:32]
        t4 = Sst[:, 0, :]
        t4b = Sst[:, 1, :]
        dve.op(lambda: V.tensor_scalar(t4, lam, -1.0, None, ALU.mult), reads=[R_par], writes=[R_chain])
        dve.op(lambda: V.tensor_tensor(t4, t4, lam, ALU.max), reads=[R_par, R_chain], writes=[R_chain])
        act.op(lambda: A.activation(t4, t4, AF.Exp, scale=-1.0), reads=[R_chain], writes=[R_chain])
        act.op(lambda: A.activation(t4, t4, AF.Ln, bias=1.0), reads=[R_chain], writes=[R_chain])
        dve.op(lambda: V.tensor_scalar(t4b, lam, -1.0, 0.0, ALU.mult, ALU.max), reads=[R_par, R_chain], writes=[R_chain])
        dve.op(lambda: V.tensor_tensor(t4, t4, t4b, ALU.add), reads=[R_chain], writes=[R_chain])
        dve.op(lambda: V.tensor_scalar(cp[l][:, :], t4, -8.0, None, ALU.mult), reads=[R_chain], writes=[R_par])
        dve.op(lambda: V.tensor_scalar(hcp[l][:, :], t4, -4.0, None, ALU.mult), reads=[R_chain], writes=[R_par])
        dve.op(lambda: V.tensor_scalar(hb[l][:, :], pp[l][:, 20:28], 0.5, None, ALU.mult), reads=[R_par], writes=[R_par])
        dve.op(lambda: V.memset(prev1[l][:, :], 0.0), writes=[R_prev1[l]])
        dve.op(lambda: V.memset(carry[l][:, :], 0.0), writes=[R_carry[l]])
    dve.op(lambda: V.memset(epsc[:, :], LN_EPS), writes=[R_par])
    ring_fill()

    lnpar = [0]

    def load_ln(l, idx, extra=()):
        sp.dma(DQ_LN, [(g_bc[:, :], lng_d[l, idx].partition_broadcast(128)),
                       (b_bc[:, :], lnb_d[l, idx].partition_broadcast(128))], writes=[R_lngb] + list(extra))

    def ln_tile(tt, bks, step, final, pool_affine=False):
        par = lnpar[0]
        lnpar[0] ^= 1
        for half, b in enumerate(bks):
            sl = xres[:, tt, half * 512:(half + 1) * 512]
            dve.op(lambda: V.scalar_tensor_tensor(sl, sl, ALPHA, ps[b][:, :], ALU.mult, ALU.add),
                   reads=[R_xres[tt], R_bank[b]], writes=[R_xres[tt]])
        for half in range(2):
            dve.op(lambda: V.bn_stats(st[par][:, half, :], xres[:, tt, half * 512:(half + 1) * 512]),
                   reads=[R_xres[tt]], writes=[R_small[par]])
        dve.op(lambda: V.bn_aggr(mv[par][:, :], st[par][:, :, :]), reads=[R_small[par]], writes=[R_small[par]])
        act.op(lambda: A.activation(rs[par][:, :], mv[par][:, 1:2], AF.Sqrt, bias=epsc[:, :]),
               reads=[R_small[par], R_par], writes=[R_small[par]])
        dve.op(lambda: V.reciprocal(rs[par][:, :], rs[par][:, :]), reads=[R_small[par]], writes=[R_small[par]])
        dve.op(lambda: V.scalar_tensor_tensor(nmr[par][:, :], mv[par][:, 0:1], -1.0, rs[par][:, :], ALU.mult, ALU.mult),
               reads=[R_small[par]], writes=[R_small[par]])
        act.op(lambda: A.activation(xres[:, tt, :], xres[:, tt, :], AF.Identity, bias=nmr[par][:, :], scale=rs[par][:, :]),
               reads=[R_xres[tt], R_small[par]], writes=[R_xres[tt]])
        aff = pool if pool_affine else dve
        AE = nc.gpsimd if pool_affine else V
        aff.op(lambda: AE.tensor_tensor(xres[:, tt, :], xres[:, tt, :], g_bc[:, :], ALU.mult),
               reads=[R_xres[tt], R_lngb], writes=[R_xres[tt]])
        aff.op(lambda: AE.tensor_tensor(xres[:, tt, :], xres[:, tt, :], b_bc[:, :], ALU.add),
               reads=[R_xres[tt], R_lngb], writes=[R_xres[tt]])
        if final:
            sp.dma(DQ_OT0 + tt, [(out_d[step, tt * 128:(tt + 1) * 128, :], xres[:, tt, :])], reads=[R_xres[tt]])
            if step + 1 < nsteps:
                sp.dma(DQ_XT0 + tt, [(xres[:, tt, :], x_d[step + 1, tt * 128:(tt + 1) * 128, :])], writes=[R_xres[tt]])

    def transposes(tt):
        for kcg in range(2):
            b = balloc()
            pe.group([(lambda q=q: P.transpose(ps[b][:, q * 128:(q + 1) * 128],
                                               xres[:, tt, (kcg * 4 + q) * 128:(kcg * 4 + q + 1) * 128], ident[:, :]))
                      for q in range(4)], reads=[R_xres[tt], R_ident], writes=[R_bank[b]])
            act.op(lambda: A.activation(xT[:, kcg * 4:(kcg + 1) * 4, tt * 128:(tt + 1) * 128],
                                        ps[b][:, :].rearrange("p (q t) -> p q t", q=4), AF.Copy),
                   reads=[R_bank[b]], writes=[R_xT[tt]])
            bfree(b)

    deferred = []

    def flush_deferred():
        for tt_ in deferred:
            transposes(tt_)
        del deferred[:]

    def proj(slot, jj, sbk, b):
        pe.group([(lambda kc=kc: P.matmul(ps[b][:, :], lhsT=ring_t[slot][:, kc, jj * 128:(jj + 1) * 128],
                                          rhs=xT[:, kc, sbk * 512:(sbk + 1) * 512], start=(kc == 0), stop=(kc == 7)))
                  for kc in range(8)], reads=[R_ring[slot]] + R_xT[sbk * 4:(sbk + 1) * 4], writes=[R_bank[b]])

    def ffn(l, k, ln_idx, step, final):
        for r in flat(R_hT):
            r.inherit(big_mixer)
        R_wd.inherit(wd_mixer)
        pool.dma(DQ_WD, [(wd3[:, 2 * m:2 * m + 2, :],
                          wdn_d[l, k][m * 256:(m + 1) * 256, :].rearrange("(j p) c -> p j c", p=128))
                         for m in range(11)], writes=[R_wd])
        load_ln(l, ln_idx)
        for s_ in range(11):
            sg = ring_take()
            su = ring_take()
            units = [(jj, sbk) for jj in range(2) for sbk in range(2)]
            if s_ == 0:
                units = [(0, 0), (1, 0), None, (0, 1), (1, 1)]
            for u_ in units:
                if u_ is None:
                    flush_deferred()
                    continue
                jj, sbk = u_
                j = 2 * s_ + jj
                if True:
                    bg, bu = balloc(), balloc()
                    proj(sg, jj, sbk, bg)
                    proj(su, jj, sbk, bu)
                    t = tmp_next()
                    act.op(lambda: A.activation(tmpA[t][:, :], ps[bg][:, :], AF.Silu), reads=[R_bank[bg]], writes=[R_tmp[t]])
                    dve.op(lambda: V.scalar_tensor_tensor(hT[:, j, sbk * 512:(sbk + 1) * 512], tmpA[t][:, :], 0.5,
                                                          ps[bu][:, :], ALU.mult, ALU.mult),
                           reads=[R_tmp[t], R_bank[bu]], writes=[R_hT[j][sbk]])
                    bfree(bg, bu)
            ring_release(2)
        order = [NTT - 1] + list(range(NTT - 1)) if k == 0 else list(range(NTT))
        hstate = None
        for i_, tt in enumerate(order):
            bks = [balloc(), balloc()]
            for half, b in enumerate(bks):
                pe.group([(lambda j=j: P.matmul(ps[b][:, :], lhsT=hT[:, j, tt * 128:(tt + 1) * 128],
                                                rhs=wd3[:, j, half * 512:(half + 1) * 512], start=(j == 0), stop=(j == NJ - 1)))
                          for j in range(NJ)], reads=[R_hT[j][tt // 4] for j in range(NJ)] + [R_wd], writes=[R_bank[b]])
            ln_tile(tt, bks, step, final)
            bfree(*bks)
            if not final and i_ >= 1:
                transposes(order[i_ - 1])
                if k == 0 and i_ == 1:
                    hstate = mixer_halo(l, step)
                    R_vt4[0].inherit([R_q])
                if k == 0 and i_ >= 2:
                    a_front(order[i_ - 2])
        if k == 0:
            vstate["pending"] = [order[-2]]
        if not final:
            deferred.append(order[-1])
        return hstate

    def ag_exchange(src_tile, in_d, out_d_, cand, R_src, R_cand):
        sp.dma(DQ_AGA, [(in_d.ap(), src_tile[:, :])], reads=[R_src], writes=[R_ag])
        pool.deps([R_ag], [R_ag])
        nc.gpsimd.collective_compute("AllGather", ALU.bypass, replica_groups=[[0, 1, 2, 3], [4, 5, 6, 7]],
                                     ins=[in_d.ap().opt()], outs=[out_d_.ap().opt()]).then_inc(ctx.sems[CC], 1)
        ctx.dcount[CC] += 1
        R_ag.w = {CC: ctx.dcount[CC]}
        R_ag.r = {}
        sp.dma(DQ_AGB, [(cand[:, :, :], out_d_.ap().rearrange("(r p) f -> p r f", p=128))], reads=[R_ag], writes=[R_cand])

    vstate = {}

    def a_front(tt):
        if "slot" not in vstate:
            vstate["slot"] = ring_take()
        sv_ = vstate["slot"]
        s_, tq = tt // 4, tt % 4
        b = balloc()
        pe.group([(lambda kc=kc: P.matmul(ps[b][:, 0:256], lhsT=xT[:, kc, tt * 128:(tt + 1) * 128],
                                          rhs=ring_t[sv_][:, kc, :], start=(kc == 0), stop=(kc == 7))) for kc in range(8)],
                 reads=[R_ring[sv_], R_xT[tt]], writes=[R_bank[b]])
        act.op(lambda: A.activation(vt4[s_][:, tq, :], ps[b][:, 0:256], GELU), reads=[R_bank[b]], writes=[R_vt4[s_]])
        bfree(b)
        dve.op(lambda: V.bn_stats(st4[s_][:, tq, :], vt4[s_][:, tq, :]), reads=[R_vt4[s_]], writes=[R_sm4[s_]])
        dve.op(lambda: V.bn_aggr(mv4[s_][:, tq, :], st4[s_][:, tq, :]), reads=[R_sm4[s_]], writes=[R_sm4[s_]])

    def mixer_halo(l, step):
        agi = step * nlayers + l
        s_xr = [ring_take(), ring_take()]
        s_cg, s_xin = ring_take(), ring_take()
        bH = balloc()

        def hproj(slot, jj, col):
            pe.group([(lambda kc=kc: P.matmul(ps[bH][:, col:col + 3], lhsT=ring_t[slot][:, kc, jj * 128:(jj + 1) * 128],
                                              rhs=xT[:, kc, NB - 3:NB], start=(kc == 0), stop=(kc == 7))) for kc in range(8)],
                     reads=[R_ring[slot], R_xT[NTT - 1]], writes=[R_bank[bH]])
        for c in range(4):
            hproj(s_xr[c // 2], c % 2, 4 * c)
        for c2 in range(2):
            hproj(s_cg, c2, 16 + 4 * c2)
            hproj(s_xin, c2, 24 + 4 * c2)
        act.op(lambda: A.activation(ex1[:, 0:12].rearrange("p (c k) -> p c k", c=4),
                                    ps[bH][:, 0:16].rearrange("p (c k) -> p c k", c=4)[:, :, 0:3], AF.Copy),
               reads=[R_bank[bH]], writes=[R_ex1])
        act.op(lambda: A.activation(tmpc[:, :].rearrange("p (c k) -> p c k", c=2),
                                    ps[bH][:, 16:24].rearrange("p (c k) -> p c k", c=2)[:, :, 1:3], AF.Copy),
               reads=[R_bank[bH]], writes=[R_tmpc])
        dve.op(lambda: V.tensor_tensor(ex1[:, 12:16].rearrange("p (c k) -> p c k", c=2), tmpc[:, :].rearrange("p (c k) -> p c k", c=2),
                                       ps[bH][:, 24:32].rearrange("p (c k) -> p c k", c=2)[:, :, 1:3], ALU.mult),
               reads=[R_tmpc, R_bank[bH]], writes=[R_ex1])
        bfree(bH)
        ag_exchange(ex1, ag1i[agi], ag1o[agi], cand1, R_ex1, R_cand1)
        return (s_xr, s_cg, s_xin)

    def mixer(l, step, hstate):
        agi = step * nlayers + l
        s_xr, s_cg, s_xin = hstate
        for r in big_mixer:
            r.inherit(flat(R_hT))
        for r in wd_mixer:
            r.inherit([R_wd])
        ppl = pp[l]
        m = sel
        sp.dma(DQ_SG, [(sgg_bc[:, :], sgg_d[l].partition_broadcast(128)),
                       (sgb_bc[:, :], sgb_d[l].partition_broadcast(128)),
                       (sgub1[:, :, :], sgub_d[l])], writes=[R_sg])
        pool.dma(DQ_BLK, [(blk1[:, :, :, :], blk_d[l])], writes=[R_blk])
        load_ln(l, 1)
        dve.op(lambda: V.tensor_scalar(hsel[:, :], prev1[l][:, :], m[:, 0:1], None, ALU.mult),
               reads=[R_prev1[l], R_par], writes=[R_hsel])
        for q in range(3):
            dve.op(lambda: V.scalar_tensor_tensor(hsel[:, :], cand1[:, q, :], m[:, q + 1:q + 2], hsel[:, :], ALU.mult, ALU.add),
                   reads=[R_cand1, R_hsel], writes=[R_hsel])
        dve.op(lambda: V.tensor_copy(prev1[l][:, :], cand1[:, 3, :]), reads=[R_cand1], writes=[R_prev1[l]])
        dve.op(lambda: V.tensor_copy(xr_buf[:, :, 0:3], hsel[:, 0:12].rearrange("p (c k) -> p c k", c=4)),
               reads=[R_hsel], writes=[R_xrh])
        for sbk in (0, 1):
            if sbk == 1:
                last_tt = deferred[-1]
                for tt_ in vstate.pop("pending"):
                    a_front(tt_)
                flush_deferred()
                a_front(last_tt)
                s_v = vstate.pop("slot")
            for c in range(4):
                b = balloc()
                proj(s_xr[c // 2], c % 2, sbk, b)
                act.op(lambda: A.activation(xr_buf[:, c, 3 + sbk * 512:3 + (sbk + 1) * 512], ps[b][:, :], AF.Copy),
                       reads=[R_bank[b]], writes=[R_xrs[sbk]])
                bfree(b)
        ring_release(2)

        def conv_c(sbk, chunks=(0, 1, 2, 3)):
            base = sbk * 512
            rd = [R_xrs[sbk], R_par] + ([R_xrs[0]] if sbk == 1 else [R_xrh])
            for c in chunks:
                o = xc_sb[sbk][:, c, :]
                en, EE, rc = (dve, V, R_xc[sbk])
                en.op(lambda: EE.tensor_scalar(o, xr_buf[:, c, base + 3:base + 515], ppl[:, 12 + c:13 + c], ppl[:, 16 + c:17 + c],
                                               ALU.mult, ALU.add), reads=rd, writes=[rc])
                for kk in range(3):
                    en.op(lambda: EE.scalar_tensor_tensor(o, xr_buf[:, c, base + kk:base + kk + 512],
                                                          ppl[:, 4 * kk + c:4 * kk + c + 1], o, ALU.mult, ALU.add),
                           reads=rd + [rc], writes=[rc])

        def xcbf_cast(sbk):
            act.op(lambda: A.activation(xcbf[:, :, :], xc_sb[sbk][:, :, :], AF.Copy), reads=[R_xc[sbk], R_xcp[sbk]], writes=[R_xcbf])

        gate_banks = {}

        def gates_pe(sbk):
            bl = []
            for c in range(4):
                br_, bi_ = balloc(), balloc()
                pe.op(lambda: P.matmul(ps[br_][:, :], lhsT=blk[l][:, 0, c, :], rhs=xcbf[:, c, :], start=True, stop=True),
                      reads=[R_xcbf, R_blk], writes=[R_bank[br_]])
                pe.op(lambda: P.matmul(ps[bi_][:, :], lhsT=blk[l][:, 1, c, :], rhs=xcbf[:, c, :], start=True, stop=True),
                      reads=[R_xcbf, R_blk], writes=[R_bank[bi_]])
                base = sbk * 512
                act.op(lambda: A.activation(a_buf[:, c, base:base + 512], ps[br_][:, :], AF.Tanh, bias=hb[l][:, c:c + 1], scale=0.5),
                       reads=[R_bank[br_], R_par], writes=[R_a[c][sbk]])
                act.op(lambda: A.activation(bx_buf[:, c, base:base + 512], ps[bi_][:, :], AF.Tanh, bias=hb[l][:, 4 + c:5 + c], scale=0.5),
                       reads=[R_bank[bi_], R_par], writes=[R_bx[c][sbk]])
                bfree(br_, bi_)

        def chain_dve_a(sbk):
            base = sbk * 512
            for c in range(4):
                bsl = bx_buf[:, c, base:base + 512]
                dve.op(lambda: V.scalar_tensor_tensor(bsl, bsl, 1.0, xc_sb[sbk][:, c, :], ALU.add, ALU.mult),
                       reads=[R_bx[c][sbk], R_xc[sbk], R_xcp[sbk]], writes=[R_bx[c][sbk]])
                dve.op(lambda: V.reduce_sum(sumr[:, c, sbk:sbk + 1], a_buf[:, c, base:base + 512], axis=AX.X),
                       reads=[R_a[c][sbk]], writes=[R_sumr])

        def chain_act(sbk):
            base = sbk * 512
            for c in range(4):
                act.op(lambda: A.activation(T_buf[:, c, :], a_buf[:, c, base:base + 512], AF.Exp, bias=cp[l][:, c:c + 1], scale=cp[l][:, c:c + 1]),
                       reads=[R_a[c][sbk], R_par], writes=[R_Tc[c]])
            for c in range(4):
                asl = a_buf[:, c, base:base + 512]
                act.op(lambda: A.activation(asl, asl, AF.Exp, bias=hcp[l][:, c:c + 1], scale=hcp[l][:, c:c + 1]),
                       reads=[R_a[c][sbk], R_sumr, R_par], writes=[R_a[c][sbk]])
            for c in range(4):
                act.op(lambda: A.activation(T_buf[:, c, :], T_buf[:, c, :], AF.Sqrt, bias=1.0, scale=-1.0), reads=[R_Tc[c]], writes=[R_Tc[c]])

        def chain_dve_b(sbk):
            base = sbk * 512
            for c in range(4):
                bsl = bx_buf[:, c, base:base + 512]
                dve.op(lambda: V.scalar_tensor_tensor(bsl, bsl, 0.5, T_buf[:, c, :], ALU.mult, ALU.mult),
                       reads=[R_bx[c][sbk], R_Tc[c]], writes=[R_bx[c][sbk]])

        def a_part1(s_):
            for half in range(2):
                bk = bv[2 * s_ + half]
                act.op(lambda: A.activation(vt4[s_][:, 2 * half:2 * half + 2, :], ps[bk][:, :].rearrange("p (q d) -> p q d", q=2), GELU),
                       reads=[R_bank[bk]], writes=[R_vt4[s_]])
                bfree(bk)

        def a_stats(s_):
            for tq in range(4):
                dve.op(lambda: V.bn_stats(st4[s_][:, tq, :], vt4[s_][:, tq, :]), reads=[R_vt4[s_]], writes=[R_sm4[s_]])
            for tq in range(4):
                dve.op(lambda: V.bn_aggr(mv4[s_][:, tq, :], st4[s_][:, tq, :]), reads=[R_sm4[s_]], writes=[R_sm4[s_]])

        def a_sqrt(s_):
            act.op(lambda: A.activation(rs4[s_][:, :], mv4[s_][:, :, 1], AF.Sqrt, bias=epsc[:, :]), reads=[R_sm4[s_], R_par], writes=[R_sm4[s_]])

        def a_part2(s_):
            dve.op(lambda: V.reciprocal(rs4[s_][:, :], rs4[s_][:, :]), reads=[R_sm4[s_]], writes=[R_sm4[s_]])
            dve.op(lambda: V.scalar_tensor_tensor(nmr4[s_][:, :], mv4[s_][:, :, 0], -1.0, rs4[s_][:, :], ALU.mult, ALU.mult),
                   reads=[R_sm4[s_]], writes=[R_sm4[s_]])
            for tq in range(4):
                dve.op(lambda: V.tensor_scalar(vt4[s_][:, tq, :], vt4[s_][:, tq, :], rs4[s_][:, tq:tq + 1], nmr4[s_][:, tq:tq + 1], ALU.mult, ALU.add),
                       reads=[R_vt4[s_], R_sm4[s_]], writes=[R_vt4[s_]])

        def a_part3(s_):
            pool.op(lambda: nc.gpsimd.memset(vlnz4[s_], 0.0), writes=[R_vl4[s_]])
            for tq in range(4):
                pool.op(lambda: nc.gpsimd.tensor_tensor(vt4[s_][:, tq, :], vt4[s_][:, tq, :], sgg_bc[:, :], ALU.mult),
                        reads=[R_vt4[s_], R_sg], writes=[R_vt4[s_]])
            sb4 = sgb_bc[:, :].rearrange("p (c h d) -> p c h d", c=2, h=2)
            for tq in range(4):
                vq = vt4[s_][:, tq, :].rearrange("p (c h d) -> p c h d", c=2, h=2)
                for hh in range(2):
                    pool.op(lambda: nc.gpsimd.tensor_tensor(vlnz4[s_][:, tq, :, hh, hh * 64:(hh + 1) * 64], vq[:, :, hh, :], sb4[:, :, hh, :], ALU.add),
                            reads=[R_vt4[s_], R_sg], writes=[R_vl4[s_]])

        def sgu_u(sbk):
            s_ = sbk
            bm = [balloc(), balloc()]
            for tq in range(4):
                for c in range(2):
                    o = ps[bm[c]][:, tq * 128:(tq + 1) * 128]
                    pe.group([lambda: P.matmul(o, lhsT=vlnz4[s_][:, tq, c, 0, :], rhs=WT[l][:, 2 * c, :], start=True, stop=False),
                              lambda: P.matmul(o, lhsT=vlnz4[s_][:, tq, c, 1, :], rhs=WT[l][:, 2 * c + 1, :], start=False, stop=False),
                              lambda: P.matmul(o, lhsT=ind2[:, :], rhs=sgub[l][:, c, :], start=False, stop=True)],
                             reads=[R_vl4[s_], R_par, R_sg], writes=[R_bank[bm[c]]])
            for c in range(2):
                bu = balloc()
                proj(s_u, c, sbk, bu)
                t = tmp_next()
                act.op(lambda: A.activation(tmpA[t][:, :], ps[bu][:, :], GELU), reads=[R_bank[bu]], writes=[R_tmp[t]])
                dve.op(lambda: V.tensor_tensor(yT[:, c, sbk * 512:(sbk + 1) * 512], tmpA[t][:, :], ps[bm[c]][:, :], ALU.mult),
                       reads=[R_tmp[t], R_bank[bm[c]]], writes=[R_yT[c][sbk]])
                bfree(bu)
            bfree(*bm)

        R_vt4[0].inherit([R_q])
        for s_ in range(2):
            R_vl4[s_].w, R_vl4[s_].r = {}, {}
            R_vl4[s_].inherit([R_p])
        a_sqrt(0)
        a_sqrt(1)
        conv_c(0)
        a_part2(0)
        a_part2(1)
        xcbf_cast(0)
        gates_pe(0)
        conv_c(1, (0, 1))
        chain_dve_a(0)
        chain_act(0)
        conv_c(1, (2, 3))
        a_part3(0)
        a_part3(1)
        xcbf_cast(1)
        gates_pe(1)
        s_u = ring_take()
        chain_dve_a(1)
        chain_dve_b(0)
        chain_act(1)
        chain_dve_b(1)
        xr_all = [R_xrs[0], R_xrs[1], R_xrh]
        for c in range(4):
            dve.op(lambda: V.tensor_tensor_scan(xr_buf[:, c, 0:1024], a_buf[:, c, :], bx_buf[:, c, :], 0.0, ALU.mult, ALU.add),
                   reads=R_a[c] + R_bx[c], writes=xr_all)
        dve.op(lambda: V.tensor_tensor(srt[:, :], sumr[:, :, 0], sumr[:, :, 1], ALU.add), reads=[R_sumr], writes=[R_chain])
        dve.op(lambda: V.tensor_scalar(srt[:, :], srt[:, :], 0.5, 0.5 * NB, ALU.mult, ALU.add), reads=[R_chain], writes=[R_chain])
        dve.op(lambda: V.tensor_tensor(srt[:, :], srt[:, :], cp[l][:, :], ALU.mult), reads=[R_chain, R_par], writes=[R_chain])
        act.op(lambda: A.activation(ex2[:, 0:4], srt[:, :], AF.Exp), reads=[R_chain], writes=[R_ex2])
        dve.op(lambda: V.tensor_copy(ex2[:, 4:8], xr_buf[:, :, 1023]), reads=xr_all, writes=[R_ex2])
        ag_exchange(ex2, ag2i[agi], ag2o[agi], cand2, R_ex2, R_cand2)
        sgu_u(0)
        sgu_u(1)
        R_p.inherit(R_vl4)
        dve.op(lambda: V.tensor_copy(pbuf[:, :, 0:2], hsel[:, 12:16].rearrange("p (c k) -> p c k", c=2)),
               reads=[R_hsel], writes=[R_p])
        for c2 in range(2):
            for sbk in range(2):
                bc_, bx_ = balloc(), balloc()
                proj(s_cg, c2, sbk, bc_)
                proj(s_xin, c2, sbk, bx_)
                t = tmp_next()
                act.op(lambda: A.activation(tmpA[t][:, :], ps[bc_][:, :], AF.Copy), reads=[R_bank[bc_]], writes=[R_tmp[t]])
                dve.op(lambda: V.tensor_tensor(pbuf[:, c2, 2 + sbk * 512:2 + (sbk + 1) * 512], tmpA[t][:, :], ps[bx_][:, :], ALU.mult),
                       reads=[R_tmp[t], R_bank[bx_]], writes=[R_p])
                bfree(bc_, bx_)
        ring_release(4)
        R_q.inherit([R_vt4[0]])
        for c in range(4):
            R_gg[c].w, R_gg[c].r = {}, {}
            R_gg[c].inherit(R_Tc if c < 2 else [R_xc[1], R_xcp[1]])
        sg = [ring_take(), ring_take()]
        for c in range(4):
            for sbk in range(2):
                b = balloc()
                proj(sg[c // 2], c % 2, sbk, b)
                act.op(lambda: A.activation(gg[c // 2][:, c % 2, sbk * 512:(sbk + 1) * 512], ps[b][:, :], GELU),
                       reads=[R_bank[b]], writes=[R_gg[c]])
                bfree(b)
        ring_release(2)
        sbg = ring_take()

        def conv_b(sbk):
            base = sbk * 512
            for c2 in range(2):
                o = qbuf[:, c2, :]
                dve.op(lambda: V.tensor_scalar(o, pbuf[:, c2, base:base + 512], ppl[:, 32 + c2:33 + c2], None, ALU.mult),
                       reads=[R_p, R_par], writes=[R_q])
                for kk in (1, 2):
                    dve.op(lambda: V.scalar_tensor_tensor(o, pbuf[:, c2, base + kk:base + kk + 512],
                                                          ppl[:, 32 + 2 * kk + c2:33 + 2 * kk + c2], o, ALU.mult, ALU.add),
                           reads=[R_p, R_q, R_par], writes=[R_q])

        def y_b(sbk):
            for c2 in range(2):
                b = balloc()
                proj(sbg, c2, sbk, b)
                dve.op(lambda: V.tensor_tensor(yT[:, 2 + c2, sbk * 512:(sbk + 1) * 512], qbuf[:, c2, :], ps[b][:, :], ALU.mult),
                       reads=[R_q, R_bank[b]], writes=[R_yT[2 + c2][sbk]])
                bfree(b)

        conv_b(0)
        prev = carry[l][:, :]
        dve.op(lambda: V.tensor_scalar(hin[:, :], prev, m[:, 0:1], None, ALU.mult), reads=[R_carry[l], R_par], writes=[R_chain])
        for q in range(4):
            sq = Sst[:, q, :]
            dve.op(lambda: V.tensor_tensor(sq, cand2[:, q, 0:4], prev, ALU.mult), reads=[R_cand2, R_chain, R_carry[l]], writes=[R_chain])
            dve.op(lambda: V.tensor_tensor(sq, sq, cand2[:, q, 4:8], ALU.add), reads=[R_cand2, R_chain], writes=[R_chain])
            if q < 3:
                dve.op(lambda: V.scalar_tensor_tensor(hin[:, :], sq, m[:, q + 1:q + 2], hin[:, :], ALU.mult, ALU.add),
                       reads=[R_chain, R_par], writes=[R_chain])
            prev = sq
        dve.op(lambda: V.tensor_copy(carry[l][:, :], Sst[:, 3, :]), reads=[R_chain], writes=[R_carry[l]])
        for c in range(4):
            dve.op(lambda: V.tensor_tensor_scan(xr_buf[:, c, 0:1024], a_buf[:, c, :], bx_buf[:, c, :], hin[:, c:c + 1], ALU.mult, ALU.add),
                   reads=R_a[c] + R_bx[c] + [R_chain], writes=xr_all)
        for sbk in range(2):
            for c in range(4):
                dve.op(lambda: V.tensor_tensor(yT[:, 4 + c, sbk * 512:(sbk + 1) * 512], gg[c // 2][:, c % 2, sbk * 512:(sbk + 1) * 512],
                                               xr_buf[:, c, sbk * 512:(sbk + 1) * 512], ALU.mult),
                       reads=[R_gg[c]] + xr_all, writes=[R_yT[4 + c][sbk]])
        y_b(0)
        conv_b(1)
        y_b(1)
        ring_release(1)
        for r in R_Tc:
            r.inherit(R_gg[0:2])
        R_xc[1].inherit(R_gg[2:4])
        R_xcp[1].inherit(R_gg[2:4])
        so = [ring_take() for _ in range(4)]
        for tt in range(NTT):
            bks = [balloc(), balloc()]
            for qd in range(4):
                o = ps[bks[qd // 2]][:, (qd % 2) * 256:(qd % 2 + 1) * 256]
                kcs = [0, 1, 4, 5, 6, 7, 2, 3]
                pe.deps([R_ring[so[qd]], R_yT[0][tt // 4]], [R_bank[bks[qd // 2]]])
                ins_ = None
                for i_k, kc in enumerate(kcs):
                    pe.deps([R_yT[kc][tt // 4]], [])
                    ins_ = P.matmul(o, lhsT=yT[:, kc, tt * 128:(tt + 1) * 128], rhs=ring_t[so[qd]][:, kc, :],
                                    start=(i_k == 0), stop=(i_k == 7))
                pe.done(ins_, [R_yT[kc][tt // 4] for kc in range(8)] + [R_ring[so[qd]]], [R_bank[bks[qd // 2]]])
            ln_tile(tt, bks, step, False, pool_affine=True)
            bfree(*bks)
            if tt >= 2:
                transposes(tt - 2)
        deferred.extend([NTT - 2, NTT - 1])
        ring_release(4)

    for step in range(nsteps):
        for tt in range(NTT):
            if tt < NTT - 2:
                transposes(tt)
            else:
                deferred.append(tt)
        for l in range(nlayers):
            hs = ffn(l, 0, 0, step, False)
            mixer(l, step, hs)
            ffn(l, 1, 2, step, l == nlayers - 1)
    if dbg:
        if "xres" in dbg_d:
            sp.dma(DQ_OUT, [(dbg_d["xres"].rearrange("(tt p) d -> p tt d", p=128), xres[:, :, :])], reads=R_xres)
    sp.wait_tok(DQ_OUT, ctx.dcount[DQ_OUT])
    for tt in range(NTT):
        sp.wait_tok(DQ_OT0 + tt, ctx.dcount[DQ_OT0 + tt])
    for e in (pe, act, dve):
        sp.wait_tok(e.idx, e.count)
    es.close()
    return nc


def _pack_small(inp, nlayers=NLAYER):
    f = np.float32
    pp = np.zeros((2, 128, NPP), f)
    for l in range(nlayers):
        cw = np.asarray(inp["rg_conv_w"][l], f)
        pp[l, :, 0:16] = cw.reshape(4, 4, 128).transpose(2, 0, 1).reshape(128, 16)
        for off, key in ((16, "rg_conv_b"), (28, "rg_lambda")):
            pp[l, :, off:off + 4] = np.asarray(inp[key][l], f).reshape(4, 128).T
        for off, key in ((20, "rg_b_a"), (24, "rg_b_i")):
            pp[l, :, off:off + 4] = np.asarray(inp[key][l], f).reshape(4, 128).T
        sw = np.asarray(inp["sconv_w"][l], f)
        pp[l, :, 32:38] = sw.reshape(3, 2, 128).transpose(2, 0, 1).reshape(128, 6)
    wsT = np.ascontiguousarray(np.asarray(inp["sgu_w"], f).transpose(0, 3, 1, 2))
    blk = np.zeros((2, 128, 2, 4, 128), f)
    for l in range(nlayers):
        for gi, key in enumerate(("rg_w_a", "rg_w_i")):
            w = np.asarray(inp[key][l], f)
            for h in range(8):
                c, hh = h // 2, h % 2
                blk[l, hh * 64:(hh + 1) * 64, gi, c, hh * 64:(hh + 1) * 64] = w[h]
    sgub = np.ascontiguousarray(np.asarray(inp["sgu_b"], f).reshape(2, 2, 2, 128).transpose(0, 2, 1, 3))
    return pp, wsT, blk, sgub


def make_in_maps(inp, nsteps=NSTEP):
    f = np.float32
    pp, wsT, blk, sgub = _pack_small(inp)
    ident = np.eye(128, dtype=f)
    triu = np.triu(np.ones((128, 128), f))
    ind2 = np.zeros((2, 128), f)
    ind2[0, :64] = 1.0
    ind2[1, 64:] = 1.0
    shared = {k: np.ascontiguousarray(np.asarray(inp[k], f)) for k in
              ("ffn_w_gate", "ffn_w_up", "ffn_w_down", "w_in", "w_out", "ln_g", "ln_b", "sgu_ln_g", "sgu_ln_b")}
    shared.update(pp=pp, wsT=wsT, blk=blk, sgub=sgub, ident=ident, triu=triu, ind2=ind2)
    x = np.asarray(inp["x"], f)
    maps = []
    for core in range(8):
        b, r = core // 4, core % 4
        xs = np.stack([x[b, (4 * i + r) * NB:(4 * i + r + 1) * NB, :] for i in range(nsteps)])
        s = np.zeros((128, 4), f)
        s[:, r] = 1.0
        m = dict(shared)
        m["x"] = np.ascontiguousarray(xs)
        m["sel"] = s
        maps.append(m)
    return maps


_NC_CACHE = {}


def kernel(**inputs):
    if "nc" not in _NC_CACHE:
        _NC_CACHE["nc"] = build()
    nc = _NC_CACHE["nc"]
    maps = make_in_maps(inputs)
    res = run_bass_kernel_spmd(nc, maps, core_ids=list(range(8)))
    out = np.empty((2, NSTEP * 4 * NB, D), np.float32)
    for core in range(8):
        b, r = core // 4, core % 4
        o = res.results[core]["out"]
        for i in range(NSTEP):
            out[b, (4 * i + r) * NB:(4 * i + r + 1) * NB, :] = o[i]
    return out
```

```python
import numpy as np
from contextlib import ExitStack
import concourse.bass as bass
import concourse.mybir as mybir
from concourse.bass_utils import run_bass_kernel_spmd

F32 = mybir.dt.float32
BF16 = mybir.dt.bfloat16
AF = mybir.ActivationFunctionType
ALU = mybir.AluOpType
AX = mybir.AxisListType

D = 1024
DFF = 2816
NJ = DFF // 128
NB = 1024
NTT = NB // 128
NSTEP = 4
NLAYER = 2
INC = 2304
ALPHA = (2.0 * NLAYER) ** 0.25
LN_EPS = 1e-5
NPP = 40
RING = 6
GELU = AF.Gelu_apprx_tanh

PE, ACT, DVE, POOL, SP = 0, 1, 2, 3, 4
DQ_RING0 = 5
DQ_WD, DQ_X, DQ_OUT, DQ_MISC, DQ_AGA, DQ_AGB, CC, DQ_LN = 11, 12, 13, 14, 15, 16, 17, 18
DQ_XT0 = 19
DQ_OT0 = 27
DQ_BLK = 35
DQ_SG = 36
NSEM = 37


class Res:
    __slots__ = ("w", "r")

    def __init__(self):
        self.w = {}
        self.r = {}

    def inherit(self, others):
        for o in others:
            for d_, s_ in ((self.w, o.w), (self.r, o.r)):
                for i, c in s_.items():
                    if d_.get(i, 0) < c:
                        d_[i] = c


class Eng:
    def __init__(self, ctx, idx, eng, same):
        self.ctx, self.idx, self.e, self.same = ctx, idx, eng, same
        self.count = 0
        self.seen = {}

    def wait_tok(self, i, c):
        if i == self.idx and not self.same:
            return
        if self.seen.get(i, 0) >= c:
            return
        self.e.wait_ge(self.ctx.sems[i], c)
        self.seen[i] = c

    def deps(self, reads, writes):
        for r in reads:
            for i, c in r.w.items():
                self.wait_tok(i, c)
        for w in writes:
            for i, c in w.w.items():
                self.wait_tok(i, c)
            for i, c in w.r.items():
                self.wait_tok(i, c)

    def done(self, ins, reads, writes):
        self.count += 1
        ins.then_inc(self.ctx.sems[self.idx], 1)
        for r in reads:
            r.r[self.idx] = self.count
        for w in writes:
            w.w = {self.idx: self.count}
            w.r = {}

    def op(self, fn, reads=(), writes=()):
        self.deps(reads, writes)
        self.done(fn(), reads, writes)

    def group(self, fns, reads=(), writes=()):
        self.deps(reads, writes)
        ins = None
        for f in fns:
            ins = f()
        self.done(ins, reads, writes)

    def dma(self, dq, pairs, reads=(), writes=()):
        self.deps(reads, writes)
        ctx = self.ctx
        for out, in_ in pairs:
            self.e.dma_start(out=out, in_=in_).then_inc(ctx.sems[dq], 16)
            ctx.dcount[dq] += 16
        c = ctx.dcount[dq]
        for r in reads:
            r.r[dq] = c
        for w in writes:
            w.w = {dq: c}
            w.r = {}


class Ctx:
    pass


def build(nsteps=NSTEP, nlayers=NLAYER, dbg=None):
    nc = bass.Bass("TRN2", target_bir_lowering=False)
    es = ExitStack()
    ctx = Ctx()
    ctx.dcount = [0] * NSEM

    def din(name, shape):
        return nc.dram_tensor(name, shape, F32, kind="ExternalInput").ap()

    x_d = din("x", [nsteps, NB, D])
    out_d = nc.dram_tensor("out", [nsteps, NB, D], F32, kind="ExternalOutput").ap()
    wg_d = din("ffn_w_gate", [2, 2, D, DFF])
    wu_d = din("ffn_w_up", [2, 2, D, DFF])
    wdn_d = din("ffn_w_down", [2, 2, DFF, D])
    win_d = din("w_in", [2, D, INC])
    wout_d = din("w_out", [2, D, D])
    lng_d = din("ln_g", [2, 3, D])
    lnb_d = din("ln_b", [2, 3, D])
    sgg_d = din("sgu_ln_g", [2, 256])
    sgb_d = din("sgu_ln_b", [2, 256])
    pp_d = din("pp", [2, 128, NPP])
    wsT_d = din("wsT", [2, 128, 4, 128])
    blk_d = din("blk", [2, 128, 2, 4, 128])
    sgub_d = din("sgub", [2, 2, 2, 128])
    ident_d = din("ident", [128, 128])
    triu_d = din("triu", [128, 128])
    ind2_d = din("ind2", [2, 128])
    sel_d = din("sel", [128, 4])
    dbg_d = {}
    if dbg:
        for name, shape in dbg.items():
            dbg_d[name] = nc.dram_tensor("dbg_" + name, shape, F32, kind="ExternalOutput").ap()

    nag = nsteps * nlayers
    ag1i = [nc.dram_tensor(f"ag1i_{i}", [128, 16], F32) for i in range(nag)]
    ag1o = [nc.dram_tensor(f"ag1o_{i}", [512, 16], F32) for i in range(nag)]
    ag2i = [nc.dram_tensor(f"ag2i_{i}", [128, 8], F32) for i in range(nag)]
    ag2o = [nc.dram_tensor(f"ag2o_{i}", [512, 8], F32) for i in range(nag)]

    def sb(name, shape, dt=F32):
        return es.enter_context(nc.sbuf_tensor(name, shape, dt))

    ctx.sems = [es.enter_context(nc.semaphore(f"s{i}")) for i in range(NSEM)]
    pe = Eng(ctx, PE, nc.tensor, False)
    act = Eng(ctx, ACT, nc.scalar, True)
    dve = Eng(ctx, DVE, nc.vector, True)
    pool = Eng(ctx, POOL, nc.gpsimd, True)
    sp = Eng(ctx, SP, nc.sync, False)

    xres = sb("xres", [128, NTT, D])
    xT = sb("xT", [128, 8, NB], BF16)
    big = sb("big", [128, NJ * NB], BF16)
    wdb = sb("wdb", [128, NJ * D], BF16)
    ring_t = [sb(f"ring{i}", [128, 8, 256], BF16) for i in range(RING)]
    tmpA = [sb(f"tmpA{i}", [128, 512]) for i in range(2)]
    g_bc = sb("g_bc", [128, D])
    b_bc = sb("b_bc", [128, D])
    ident = sb("ident_sb", [128, 128])
    ind2 = sb("ind2_sb", [2, 128])
    sel = sb("sel_sb", [128, 4])
    epsc = sb("epsc", [128, 1])
    pbuf = sb("pbuf", [128, 2, 1032])
    st = [sb(f"st{i}", [128, 2, 6]) for i in range(2)]
    mv = [sb(f"mv{i}", [128, 2]) for i in range(2)]
    rs = [sb(f"rs{i}", [128, 1]) for i in range(2)]
    nmr = [sb(f"nmr{i}", [128, 1]) for i in range(2)]
    ex1 = sb("ex1", [128, 16])
    cand1 = sb("cand1", [128, 4, 16])
    hsel = sb("hsel", [128, 16])
    ex2 = sb("ex2", [128, 8])
    cand2 = sb("cand2", [128, 4, 8])
    sumr = sb("sumr", [128, 4, 2])
    srt = sb("srt", [128, 4])
    Sst = sb("Sst", [128, 4, 4])
    hin = sb("hin", [128, 4])
    tmpc = sb("tmpc", [128, 4])
    st4 = [sb(f"st4_{i}", [128, 4, 6]) for i in range(2)]
    mv4 = [sb(f"mv4_{i}", [128, 4, 2]) for i in range(2)]
    rs4 = [sb(f"rs4_{i}", [128, 4]) for i in range(2)]
    nmr4 = [sb(f"nmr4_{i}", [128, 4]) for i in range(2)]
    pp = [sb(f"pp{l}", [128, NPP]) for l in range(nlayers)]
    cp = [sb(f"cp{l}", [128, 4]) for l in range(nlayers)]
    hcp = [sb(f"hcp{l}", [128, 4]) for l in range(nlayers)]
    hb = [sb(f"hb{l}", [128, 8]) for l in range(nlayers)]
    WT = [sb(f"WT{l}", [128, 4, 128], BF16) for l in range(nlayers)]
    blk1 = sb("blk_sb", [128, 2, 4, 128], BF16)
    blk = [blk1 for l in range(nlayers)]
    sgg_bc = sb("sgg", [128, 256])
    sgb_bc = sb("sgb", [128, 256])
    vt4 = [sb(f"vt4_{i}", [128, 4, 256]) for i in range(2)]
    xc0 = sb("xc0", [128, 4, 512])
    sgub1 = sb("sgub_sb", [2, 2, 128])
    sgub = [sgub1 for l in range(nlayers)]
    prev1 = [sb(f"prev1_{l}", [128, 16]) for l in range(nlayers)]
    carry = [sb(f"carry{l}", [128, 4]) for l in range(nlayers)]
    ps = [es.enter_context(nc.psum_tensor(f"ps{i}", [128, 512], F32)) for i in range(8)]

    hT = big[:, :].rearrange("p (j t) -> p j t", j=NJ)
    wd3 = wdb[:, :].rearrange("p (j c) -> p j c", j=NJ)

    def fview(region, boff, n, c=None):
        v = region[:, boff // 2: boff // 2 + 2 * n].bitcast(F32)
        if c:
            v = v.rearrange("p (c t) -> p c t", c=c)
        return v

    a_buf = fview(big, 0, 4096, 4)
    bx_buf = fview(big, 16384, 4096, 4)
    T_buf = fview(big, 32768, 2048, 4)
    xcbf = big[:, 40960 // 2: 40960 // 2 + 2048].rearrange("p (c t) -> p c t", c=4)
    xr_buf = fview(wdb, 0, 4128, 4)
    xc_buf = fview(wdb, 16512, 2048, 4)
    yT = wdb[:, 24704 // 2: 24704 // 2 + 8192].rearrange("p (c t) -> p c t", c=8)
    xc_sb = [xc0[:, :, :], xc_buf]
    qbuf = vt4[0][:, :, :].rearrange("p q d -> p (q d)").rearrange("p (c t) -> p c t", c=2)
    pflat = pbuf[:, :, :].rearrange("p c t -> p (c t)").bitcast(BF16)
    vlnz4 = [pflat[:, i * 2048:(i + 1) * 2048].rearrange("p (q c h d) -> p q c h d", q=4, c=2, h=2) for i in range(2)]
    gg = [fview(big, 32768, 2048, 2), fview(wdb, 16512, 2048, 2)]

    R_xres = [Res() for _ in range(NTT)]
    R_xT = [Res() for _ in range(NTT)]
    R_hT = [[Res(), Res()] for _ in range(NJ)]
    R_wd = Res()
    R_ring = [Res() for _ in range(RING)]
    R_tmp = [Res(), Res()]
    R_lngb = Res()
    R_small = [Res(), Res()]
    R_bank = [Res() for _ in range(8)]
    R_par = Res()
    R_a = [[Res(), Res()] for _ in range(4)]
    R_bx = [[Res(), Res()] for _ in range(4)]
    R_Tc = [Res() for _ in range(4)]
    R_xcbf = Res()
    R_xc = [Res(), Res()]
    R_xcp = [Res(), Res()]
    R_sg = Res()
    R_ident = Res()
    R_eps = Res()
    R_stg = Res()
    R_blk = Res()
    R_xrs = [Res(), Res()]
    R_xrh = Res()
    R_tmpc = Res()
    R_vt4 = [Res(), Res()]
    R_vl4 = [Res(), Res()]
    R_sm4 = [Res(), Res()]
    R_gg = [Res() for _ in range(4)]
    R_yT = [[Res(), Res()] for _ in range(8)]
    R_p, R_q = Res(), Res()
    R_ex1, R_cand1, R_hsel, R_ex2, R_cand2 = Res(), Res(), Res(), Res(), Res()
    R_sumr, R_chain = Res(), Res()
    R_prev1 = [Res() for _ in range(nlayers)]
    R_carry = [Res() for _ in range(nlayers)]
    R_ag = Res()

    def flat(ll):
        return [r for sub in ll for r in sub]

    big_mixer = flat(R_a) + flat(R_bx) + R_Tc + [R_xcbf]
    wd_mixer = R_xrs + [R_xrh, R_xc[1], R_xcp[1]] + flat(R_yT)

    live = [False] * 8
    nxt = [0]

    def balloc():
        for k in range(8):
            b = (nxt[0] + k) % 8
            if not live[b]:
                live[b] = True
                nxt[0] = (b + 1) % 8
                return b
        raise RuntimeError("no free PSUM bank")

    def bfree(*bs):
        for b in bs:
            live[b] = False

    tmpi = [0]

    def tmp_next():
        tmpi[0] ^= 1
        return tmpi[0]

    loads = []
    for st_ in range(nsteps):
        for l in range(nlayers):
            for k in (0, None, 1):
                if k is None:
                    for c0 in (1792, 2048, 768, 1024, 256, 0, 512, 1280, 1536):
                        loads.append(win_d[l][:, c0:c0 + 256])
                    for c0 in (0, 256, 512, 768):
                        loads.append(wout_d[l][:, c0:c0 + 256])
                else:
                    for s_ in range(11):
                        loads.append(wg_d[l, k][:, s_ * 256:(s_ + 1) * 256])
                        loads.append(wu_d[l, k][:, s_ * 256:(s_ + 1) * 256])
    rstate = {"issued": 0, "consumed": 0, "taken": 0}

    def ring_fill():
        while rstate["issued"] < len(loads) and rstate["issued"] - rstate["consumed"] < RING:
            n = rstate["issued"]
            s_ = n % RING
            src = loads[n].rearrange("(kc p) c -> p kc c", p=128)
            pool.dma(DQ_RING0 + s_, [(ring_t[s_][:, :, :], src)], writes=[R_ring[s_]])
            rstate["issued"] += 1

    def ring_take():
        n = rstate["taken"]
        assert n < rstate["issued"], "ring underflow"
        rstate["taken"] += 1
        return n % RING

    def ring_release(k=1):
        rstate["consumed"] += k
        ring_fill()

    V, A, P = nc.vector, nc.scalar, nc.tensor

    ring_fill()
    sp.dma(DQ_X, [(ident[:, :], ident_d[:, :])], writes=[R_ident])
    for tt in range(NTT):
        sp.dma(DQ_XT0 + tt, [(xres[:, tt, :], x_d[0, tt * 128:(tt + 1) * 128, :])], writes=[R_xres[tt]])
    sp.dma(DQ_MISC, [(ind2[:, :], ind2_d[:, :]), (sel[:, :], sel_d[:, :])] +
           [(pp[l][:, :], pp_d[l]) for l in range(nlayers)], writes=[R_par])
    dve.op(lambda: V.memset(epsc[:, :], LN_EPS), writes=[R_eps])

    def setup_compute():
        triu_sb = xc0[:, 0, 0:128]
        sp.dma(DQ_MISC, [(triu_sb, triu_d[:, :])], writes=[R_xc[0]])
        for l in range(nlayers):
            stg = xc0[:, 1, :]
            sp.dma(DQ_MISC, [(stg.rearrange("p (h t) -> p h t", h=4), wsT_d[l])], writes=[R_stg])
            for h in range(4):
                dve.op(lambda: V.tensor_tensor(WT[l][:, h, :], stg[:, h * 128:(h + 1) * 128], triu_sb, ALU.mult),
                       reads=[R_stg, R_xc[0], R_par], writes=[R_par])
            lam = pp[l][:, 28:32]
            t4 = Sst[:, 0, :]
            t4b = Sst[:, 1, :]
            dve.op(lambda: V.tensor_scalar(t4, lam, -1.0, None, ALU.mult), reads=[R_par], writes=[R_chain])
            dve.op(lambda: V.tensor_tensor(t4, t4, lam, ALU.max), reads=[R_par, R_chain], writes=[R_chain])
            act.op(lambda: A.activation(t4, t4, AF.Exp, scale=-1.0), reads=[R_chain], writes=[R_chain])
            act.op(lambda: A.activation(t4, t4, AF.Ln, bias=1.0), reads=[R_chain], writes=[R_chain])
            dve.op(lambda: V.tensor_scalar(t4b, lam, -1.0, 0.0, ALU.mult, ALU.max), reads=[R_par, R_chain], writes=[R_chain])
            dve.op(lambda: V.tensor_tensor(t4, t4, t4b, ALU.add), reads=[R_chain], writes=[R_chain])
            dve.op(lambda: V.tensor_scalar(cp[l][:, :], t4, -8.0, None, ALU.mult), reads=[R_chain], writes=[R_par])
            dve.op(lambda: V.tensor_scalar(hcp[l][:, :], t4, -4.0, None, ALU.mult), reads=[R_chain], writes=[R_par])
            dve.op(lambda: V.tensor_scalar(hb[l][:, :], pp[l][:, 20:28], 0.5, None, ALU.mult), reads=[R_par], writes=[R_par])
            dve.op(lambda: V.memset(prev1[l][:, :], 0.0), writes=[R_prev1[l]])
            dve.op(lambda: V.memset(carry[l][:, :], 0.0), writes=[R_carry[l]])
        R_xc[0].inherit([R_stg])

    lnpar = [0]

    def load_ln(l, idx, extra=()):
        sp.dma(DQ_LN, [(g_bc[:, :], lng_d[l, idx].partition_broadcast(128)),
                       (b_bc[:, :], lnb_d[l, idx].partition_broadcast(128))], writes=[R_lngb] + list(extra))

    def ln_tile(tt, bks, step, final, pool_affine=False):
        par = lnpar[0]
        lnpar[0] ^= 1
        for half, b in enumerate(bks):
            sl = xres[:, tt, half * 512:(half + 1) * 512]
            dve.op(lambda: V.scalar_tensor_tensor(sl, sl, ALPHA, ps[b][:, :], ALU.mult, ALU.add),
                   reads=[R_xres[tt], R_bank[b]], writes=[R_xres[tt]])
        for half in range(2):
            dve.op(lambda: V.bn_stats(st[par][:, half, :], xres[:, tt, half * 512:(half + 1) * 512]),
                   reads=[R_xres[tt]], writes=[R_small[par]])
        dve.op(lambda: V.bn_aggr(mv[par][:, :], st[par][:, :, :]), reads=[R_small[par]], writes=[R_small[par]])
        act.op(lambda: A.activation(rs[par][:, :], mv[par][:, 1:2], AF.Sqrt, bias=epsc[:, :]),
               reads=[R_small[par], R_eps], writes=[R_small[par]])
        dve.op(lambda: V.reciprocal(rs[par][:, :], rs[par][:, :]), reads=[R_small[par]], writes=[R_small[par]])
        dve.op(lambda: V.scalar_tensor_tensor(nmr[par][:, :], mv[par][:, 0:1], -1.0, rs[par][:, :], ALU.mult, ALU.mult),
               reads=[R_small[par]], writes=[R_small[par]])
        act.op(lambda: A.activation(xres[:, tt, :], xres[:, tt, :], AF.Identity, bias=nmr[par][:, :], scale=rs[par][:, :]),
               reads=[R_xres[tt], R_small[par]], writes=[R_xres[tt]])
        aff = pool if pool_affine else dve
        AE = nc.gpsimd if pool_affine else V
        aff.op(lambda: AE.tensor_tensor(xres[:, tt, :], xres[:, tt, :], g_bc[:, :], ALU.mult),
               reads=[R_xres[tt], R_lngb], writes=[R_xres[tt]])
        aff.op(lambda: AE.tensor_tensor(xres[:, tt, :], xres[:, tt, :], b_bc[:, :], ALU.add),
               reads=[R_xres[tt], R_lngb], writes=[R_xres[tt]])
        if final:
            sp.dma(DQ_OT0 + tt, [(out_d[step, tt * 128:(tt + 1) * 128, :], xres[:, tt, :])], reads=[R_xres[tt]])
            if step + 1 < nsteps:
                sp.dma(DQ_XT0 + tt, [(xres[:, tt, :], x_d[step + 1, tt * 128:(tt + 1) * 128, :])], writes=[R_xres[tt]])

    def transposes(tt):
        for kcg in range(2):
            b = balloc()
            pe.group([(lambda q=q: P.transpose(ps[b][:, q * 128:(q + 1) * 128],
                                               xres[:, tt, (kcg * 4 + q) * 128:(kcg * 4 + q + 1) * 128], ident[:, :]))
                      for q in range(4)], reads=[R_xres[tt], R_ident], writes=[R_bank[b]])
            act.op(lambda: A.activation(xT[:, kcg * 4:(kcg + 1) * 4, tt * 128:(tt + 1) * 128],
                                        ps[b][:, :].rearrange("p (q t) -> p q t", q=4), AF.Copy),
                   reads=[R_bank[b]], writes=[R_xT[tt]])
            bfree(b)

    deferred = []

    def flush_deferred():
        for tt_ in deferred:
            transposes(tt_)
        del deferred[:]

    def proj(slot, jj, sbk, b):
        pe.group([(lambda kc=kc: P.matmul(ps[b][:, :], lhsT=ring_t[slot][:, kc, jj * 128:(jj + 1) * 128],
                                          rhs=xT[:, kc, sbk * 512:(sbk + 1) * 512], start=(kc == 0), stop=(kc == 7)))
                  for kc in range(8)], reads=[R_ring[slot]] + R_xT[sbk * 4:(sbk + 1) * 4], writes=[R_bank[b]])

    def ffn(l, k, ln_idx, step, final):
        for r in flat(R_hT):
            r.inherit(big_mixer)
        R_wd.inherit(wd_mixer)
        pool.dma(DQ_WD, [(wd3[:, 2 * m:2 * m + 2, :],
                          wdn_d[l, k][m * 256:(m + 1) * 256, :].rearrange("(j p) c -> p j c", p=128))
                         for m in range(11)], writes=[R_wd])
        load_ln(l, ln_idx)
        for s_ in range(11):
            sg = ring_take()
            su = ring_take()
            units = [(jj, sbk) for jj in range(2) for sbk in range(2)]
            if s_ == 0:
                units = [(0, 0), (1, 0), None, (0, 1), (1, 1)]
            for u_ in units:
                if u_ is None:
                    flush_deferred()
                    continue
                jj, sbk = u_
                j = 2 * s_ + jj
                if True:
                    bg, bu = balloc(), balloc()
                    proj(sg, jj, sbk, bg)
                    proj(su, jj, sbk, bu)
                    t = tmp_next()
                    act.op(lambda: A.activation(tmpA[t][:, :], ps[bg][:, :], AF.Silu), reads=[R_bank[bg]], writes=[R_tmp[t]])
                    dve.op(lambda: V.scalar_tensor_tensor(hT[:, j, sbk * 512:(sbk + 1) * 512], tmpA[t][:, :], 0.5,
                                                          ps[bu][:, :], ALU.mult, ALU.mult),
                           reads=[R_tmp[t], R_bank[bu]], writes=[R_hT[j][sbk]])
                    bfree(bg, bu)
            ring_release(2)
        if step == 0 and l == 0 and k == 0:
            setup_compute()
        order = [NTT - 1] + list(range(NTT - 1)) if k == 0 else list(range(NTT))
        hstate = None
        for i_, tt in enumerate(order):
            bks = [balloc(), balloc()]
            for half, b in enumerate(bks):
                pe.group([(lambda j=j: P.matmul(ps[b][:, :], lhsT=hT[:, j, tt * 128:(tt + 1) * 128],
                                                rhs=wd3[:, j, half * 512:(half + 1) * 512], start=(j == 0), stop=(j == NJ - 1)))
                          for j in range(NJ)], reads=[R_hT[j][tt // 4] for j in range(NJ)] + [R_wd], writes=[R_bank[b]])
            ln_tile(tt, bks, step, final)
            bfree(*bks)
            if not final and i_ >= 1:
                transposes(order[i_ - 1])
                if k == 0 and i_ == 1:
                    hstate = mixer_halo(l, step)
                    R_vt4[0].inherit([R_q])
                if k == 0 and i_ >= 2:
                    a_front(order[i_ - 2])
        if k == 0:
            vstate["pending"] = [order[-2]]
        if not final:
            deferred.append(order[-1])
        return hstate

    def ag_exchange(src_tile, in_d, out_d_, cand, R_src, R_cand):
        sp.dma(DQ_AGA, [(in_d.ap(), src_tile[:, :])], reads=[R_src], writes=[R_ag])
        pool.deps([R_ag], [R_ag])
        nc.gpsimd.collective_compute("AllGather", ALU.bypass, replica_groups=[[0, 1, 2, 3], [4, 5, 6, 7]],
                                     ins=[in_d.ap().opt()], outs=[out_d_.ap().opt()]).then_inc(ctx.sems[CC], 1)
        ctx.dcount[CC] += 1
        R_ag.w = {CC: ctx.dcount[CC]}
        R_ag.r = {}
        sp.dma(DQ_AGB, [(cand[:, :, :], out_d_.ap().rearrange("(r p) f -> p r f", p=128))], reads=[R_ag], writes=[R_cand])

    vstate = {}

    def a_front(tt):
        if "slot" not in vstate:
            vstate["slot"] = ring_take()
        sv_ = vstate["slot"]
        s_, tq = tt // 4, tt % 4
        b = balloc()
        pe.group([(lambda kc=kc: P.matmul(ps[b][:, 0:256], lhsT=xT[:, kc, tt * 128:(tt + 1) * 128],
                                          rhs=ring_t[sv_][:, kc, :], start=(kc == 0), stop=(kc == 7))) for kc in range(8)],
                 reads=[R_ring[sv_], R_xT[tt]], writes=[R_bank[b]])
        act.op(lambda: A.activation(vt4[s_][:, tq, :], ps[b][:, 0:256], GELU), reads=[R_bank[b]], writes=[R_vt4[s_]])
        bfree(b)
        dve.op(lambda: V.bn_stats(st4[s_][:, tq, :], vt4[s_][:, tq, :]), reads=[R_vt4[s_]], writes=[R_sm4[s_]])
        dve.op(lambda: V.bn_aggr(mv4[s_][:, tq, :], st4[s_][:, tq, :]), reads=[R_sm4[s_]], writes=[R_sm4[s_]])

    def mixer_halo(l, step):
        agi = step * nlayers + l
        s_xr = [ring_take(), ring_take()]
        s_cg, s_xin = ring_take(), ring_take()
        bH = balloc()

        def hproj(slot, jj, col):
            pe.group([(lambda kc=kc: P.matmul(ps[bH][:, col:col + 3], lhsT=ring_t[slot][:, kc, jj * 128:(jj + 1) * 128],
                                              rhs=xT[:, kc, NB - 3:NB], start=(kc == 0), stop=(kc == 7))) for kc in range(8)],
                     reads=[R_ring[slot], R_xT[NTT - 1]], writes=[R_bank[bH]])
        for c in range(4):
            hproj(s_xr[c // 2], c % 2, 4 * c)
        for c2 in range(2):
            hproj(s_cg, c2, 16 + 4 * c2)
            hproj(s_xin, c2, 24 + 4 * c2)
        act.op(lambda: A.activation(ex1[:, 0:12].rearrange("p (c k) -> p c k", c=4),
                                    ps[bH][:, 0:16].rearrange("p (c k) -> p c k", c=4)[:, :, 0:3], AF.Copy),
               reads=[R_bank[bH]], writes=[R_ex1])
        act.op(lambda: A.activation(tmpc[:, :].rearrange("p (c k) -> p c k", c=2),
                                    ps[bH][:, 16:24].rearrange("p (c k) -> p c k", c=2)[:, :, 1:3], AF.Copy),
               reads=[R_bank[bH]], writes=[R_tmpc])
        dve.op(lambda: V.tensor_tensor(ex1[:, 12:16].rearrange("p (c k) -> p c k", c=2), tmpc[:, :].rearrange("p (c k) -> p c k", c=2),
                                       ps[bH][:, 24:32].rearrange("p (c k) -> p c k", c=2)[:, :, 1:3], ALU.mult),
               reads=[R_tmpc, R_bank[bH]], writes=[R_ex1])
        bfree(bH)
        ag_exchange(ex1, ag1i[agi], ag1o[agi], cand1, R_ex1, R_cand1)
        return (s_xr, s_cg, s_xin)

    def mixer(l, step, hstate):
        agi = step * nlayers + l
        s_xr, s_cg, s_xin = hstate
        for r in big_mixer:
            r.inherit(flat(R_hT))
        for r in wd_mixer:
            r.inherit([R_wd])
        ppl = pp[l]
        m = sel
        sp.dma(DQ_SG, [(sgg_bc[:, :], sgg_d[l].partition_broadcast(128)),
                       (sgb_bc[:, :], sgb_d[l].partition_broadcast(128)),
                       (sgub1[:, :, :], sgub_d[l])], writes=[R_sg])
        pool.dma(DQ_BLK, [(blk1[:, :, :, :], blk_d[l])], writes=[R_blk])
        load_ln(l, 1)
        dve.op(lambda: V.tensor_scalar(hsel[:, :], prev1[l][:, :], m[:, 0:1], None, ALU.mult),
               reads=[R_prev1[l], R_par], writes=[R_hsel])
        for q in range(3):
            dve.op(lambda: V.scalar_tensor_tensor(hsel[:, :], cand1[:, q, :], m[:, q + 1:q + 2], hsel[:, :], ALU.mult, ALU.add),
                   reads=[R_cand1, R_hsel], writes=[R_hsel])
        dve.op(lambda: V.tensor_copy(prev1[l][:, :], cand1[:, 3, :]), reads=[R_cand1], writes=[R_prev1[l]])
        dve.op(lambda: V.tensor_copy(xr_buf[:, :, 0:3], hsel[:, 0:12].rearrange("p (c k) -> p c k", c=4)),
               reads=[R_hsel], writes=[R_xrh])
        for sbk in (0, 1):
            if sbk == 1:
                last_tt = deferred[-1]
                for tt_ in vstate.pop("pending"):
                    a_front(tt_)
                flush_deferred()
                a_front(last_tt)
                s_v = vstate.pop("slot")
            for c in range(4):
                b = balloc()
                proj(s_xr[c // 2], c % 2, sbk, b)
                act.op(lambda: A.activation(xr_buf[:, c, 3 + sbk * 512:3 + (sbk + 1) * 512], ps[b][:, :], AF.Copy),
                       reads=[R_bank[b]], writes=[R_xrs[sbk]])
                bfree(b)
        ring_release(2)

        def conv_c(sbk, chunks=(0, 1, 2, 3)):
            base = sbk * 512
            rd = [R_xrs[sbk], R_par] + ([R_xrs[0]] if sbk == 1 else [R_xrh])
            for c in chunks:
                o = xc_sb[sbk][:, c, :]
                en, EE, rc = (dve, V, R_xc[sbk])
                en.op(lambda: EE.tensor_scalar(o, xr_buf[:, c, base + 3:base + 515], ppl[:, 12 + c:13 + c], ppl[:, 16 + c:17 + c],
                                               ALU.mult, ALU.add), reads=rd, writes=[rc])
                for kk in range(3):
                    en.op(lambda: EE.scalar_tensor_tensor(o, xr_buf[:, c, base + kk:base + kk + 512],
                                                          ppl[:, 4 * kk + c:4 * kk + c + 1], o, ALU.mult, ALU.add),
                           reads=rd + [rc], writes=[rc])

        def xcbf_cast(sbk):
            act.op(lambda: A.activation(xcbf[:, :, :], xc_sb[sbk][:, :, :], AF.Copy), reads=[R_xc[sbk], R_xcp[sbk]], writes=[R_xcbf])

        gate_banks = {}

        def gates_pe(sbk):
            bl = []
            for c in range(4):
                br_, bi_ = balloc(), balloc()
                pe.op(lambda: P.matmul(ps[br_][:, :], lhsT=blk[l][:, 0, c, :], rhs=xcbf[:, c, :], start=True, stop=True),
                      reads=[R_xcbf, R_blk], writes=[R_bank[br_]])
                pe.op(lambda: P.matmul(ps[bi_][:, :], lhsT=blk[l][:, 1, c, :], rhs=xcbf[:, c, :], start=True, stop=True),
                      reads=[R_xcbf, R_blk], writes=[R_bank[bi_]])
                base = sbk * 512
                act.op(lambda: A.activation(a_buf[:, c, base:base + 512], ps[br_][:, :], AF.Tanh, bias=hb[l][:, c:c + 1], scale=0.5),
                       reads=[R_bank[br_], R_par], writes=[R_a[c][sbk]])
                act.op(lambda: A.activation(bx_buf[:, c, base:base + 512], ps[bi_][:, :], AF.Tanh, bias=hb[l][:, 4 + c:5 + c], scale=0.5),
                       reads=[R_bank[bi_], R_par], writes=[R_bx[c][sbk]])
                bfree(br_, bi_)

        def chain_dve_a(sbk):
            base = sbk * 512
            for c in range(4):
                bsl = bx_buf[:, c, base:base + 512]
                dve.op(lambda: V.scalar_tensor_tensor(bsl, bsl, 1.0, xc_sb[sbk][:, c, :], ALU.add, ALU.mult),
                       reads=[R_bx[c][sbk], R_xc[sbk], R_xcp[sbk]], writes=[R_bx[c][sbk]])
                dve.op(lambda: V.reduce_sum(sumr[:, c, sbk:sbk + 1], a_buf[:, c, base:base + 512], axis=AX.X),
                       reads=[R_a[c][sbk]], writes=[R_sumr])

        def chain_act(sbk):
            base = sbk * 512
            for c in range(4):
                act.op(lambda: A.activation(T_buf[:, c, :], a_buf[:, c, base:base + 512], AF.Exp, bias=cp[l][:, c:c + 1], scale=cp[l][:, c:c + 1]),
                       reads=[R_a[c][sbk], R_par], writes=[R_Tc[c]])
            for c in range(4):
                asl = a_buf[:, c, base:base + 512]
                act.op(lambda: A.activation(asl, asl, AF.Exp, bias=hcp[l][:, c:c + 1], scale=hcp[l][:, c:c + 1]),
                       reads=[R_a[c][sbk], R_sumr, R_par], writes=[R_a[c][sbk]])
            for c in range(4):
                act.op(lambda: A.activation(T_buf[:, c, :], T_buf[:, c, :], AF.Sqrt, bias=1.0, scale=-1.0), reads=[R_Tc[c]], writes=[R_Tc[c]])

        def chain_dve_b(sbk):
            base = sbk * 512
            for c in range(4):
                bsl = bx_buf[:, c, base:base + 512]
                dve.op(lambda: V.scalar_tensor_tensor(bsl, bsl, 0.5, T_buf[:, c, :], ALU.mult, ALU.mult),
                       reads=[R_bx[c][sbk], R_Tc[c]], writes=[R_bx[c][sbk]])

        def a_part1(s_):
            for half in range(2):
                bk = bv[2 * s_ + half]
                act.op(lambda: A.activation(vt4[s_][:, 2 * half:2 * half + 2, :], ps[bk][:, :].rearrange("p (q d) -> p q d", q=2), GELU),
                       reads=[R_bank[bk]], writes=[R_vt4[s_]])
                bfree(bk)

        def a_stats(s_):
            for tq in range(4):
                dve.op(lambda: V.bn_stats(st4[s_][:, tq, :], vt4[s_][:, tq, :]), reads=[R_vt4[s_]], writes=[R_sm4[s_]])
            for tq in range(4):
                dve.op(lambda: V.bn_aggr(mv4[s_][:, tq, :], st4[s_][:, tq, :]), reads=[R_sm4[s_]], writes=[R_sm4[s_]])

        def a_sqrt(s_):
            act.op(lambda: A.activation(rs4[s_][:, :], mv4[s_][:, :, 1], AF.Sqrt, bias=epsc[:, :]), reads=[R_sm4[s_], R_eps], writes=[R_sm4[s_]])

        def a_part2(s_):
            dve.op(lambda: V.reciprocal(rs4[s_][:, :], rs4[s_][:, :]), reads=[R_sm4[s_]], writes=[R_sm4[s_]])
            dve.op(lambda: V.scalar_tensor_tensor(nmr4[s_][:, :], mv4[s_][:, :, 0], -1.0, rs4[s_][:, :], ALU.mult, ALU.mult),
                   reads=[R_sm4[s_]], writes=[R_sm4[s_]])
            for tq in range(4):
                dve.op(lambda: V.tensor_scalar(vt4[s_][:, tq, :], vt4[s_][:, tq, :], rs4[s_][:, tq:tq + 1], nmr4[s_][:, tq:tq + 1], ALU.mult, ALU.add),
                       reads=[R_vt4[s_], R_sm4[s_]], writes=[R_vt4[s_]])

        def a_part3(s_):
            pool.op(lambda: nc.gpsimd.memset(vlnz4[s_], 0.0), writes=[R_vl4[s_]])
            for tq in range(4):
                pool.op(lambda: nc.gpsimd.tensor_tensor(vt4[s_][:, tq, :], vt4[s_][:, tq, :], sgg_bc[:, :], ALU.mult),
                        reads=[R_vt4[s_], R_sg], writes=[R_vt4[s_]])
            sb4 = sgb_bc[:, :].rearrange("p (c h d) -> p c h d", c=2, h=2)
            for tq in range(4):
                vq = vt4[s_][:, tq, :].rearrange("p (c h d) -> p c h d", c=2, h=2)
                for hh in range(2):
                    pool.op(lambda: nc.gpsimd.tensor_tensor(vlnz4[s_][:, tq, :, hh, hh * 64:(hh + 1) * 64], vq[:, :, hh, :], sb4[:, :, hh, :], ALU.add),
                            reads=[R_vt4[s_], R_sg], writes=[R_vl4[s_]])

        def sgu_u(sbk):
            s_ = sbk
            bm = [balloc(), balloc()]
            for tq in range(4):
                for c in range(2):
                    o = ps[bm[c]][:, tq * 128:(tq + 1) * 128]
                    pe.group([lambda: P.matmul(o, lhsT=vlnz4[s_][:, tq, c, 0, :], rhs=WT[l][:, 2 * c, :], start=True, stop=False),
                              lambda: P.matmul(o, lhsT=vlnz4[s_][:, tq, c, 1, :], rhs=WT[l][:, 2 * c + 1, :], start=False, stop=False),
                              lambda: P.matmul(o, lhsT=ind2[:, :], rhs=sgub[l][:, c, :], start=False, stop=True)],
                             reads=[R_vl4[s_], R_par, R_sg], writes=[R_bank[bm[c]]])
            for c in range(2):
                bu = balloc()
                proj(s_u, c, sbk, bu)
                t = tmp_next()
                act.op(lambda: A.activation(tmpA[t][:, :], ps[bu][:, :], GELU), reads=[R_bank[bu]], writes=[R_tmp[t]])
                dve.op(lambda: V.tensor_tensor(yT[:, c, sbk * 512:(sbk + 1) * 512], tmpA[t][:, :], ps[bm[c]][:, :], ALU.mult),
                       reads=[R_tmp[t], R_bank[bm[c]]], writes=[R_yT[c][sbk]])
                bfree(bu)
            bfree(*bm)

        R_vt4[0].inherit([R_q])
        for s_ in range(2):
            R_vl4[s_].w, R_vl4[s_].r = {}, {}
            R_vl4[s_].inherit([R_p])
        a_sqrt(0)
        a_sqrt(1)
        conv_c(0)
        a_part2(0)
        a_part2(1)
        xcbf_cast(0)
        gates_pe(0)
        conv_c(1, (0, 1))
        chain_dve_a(0)
        chain_act(0)
        conv_c(1, (2, 3))
        a_part3(0)
        a_part3(1)
        xcbf_cast(1)
        gates_pe(1)
        s_u = ring_take()
        chain_dve_a(1)
        chain_dve_b(0)
        chain_act(1)
        chain_dve_b(1)
        xr_all = [R_xrs[0], R_xrs[1], R_xrh]
        for c in range(4):
            dve.op(lambda: V.tensor_tensor_scan(xr_buf[:, c, 0:1024], a_buf[:, c, :], bx_buf[:, c, :], 0.0, ALU.mult, ALU.add),
                   reads=R_a[c] + R_bx[c], writes=xr_all)
        dve.op(lambda: V.tensor_tensor(srt[:, :], sumr[:, :, 0], sumr[:, :, 1], ALU.add), reads=[R_sumr], writes=[R_chain])
        dve.op(lambda: V.tensor_scalar(srt[:, :], srt[:, :], 0.5, 0.5 * NB, ALU.mult, ALU.add), reads=[R_chain], writes=[R_chain])
        dve.op(lambda: V.tensor_tensor(srt[:, :], srt[:, :], cp[l][:, :], ALU.mult), reads=[R_chain, R_par], writes=[R_chain])
        act.op(lambda: A.activation(ex2[:, 0:4], srt[:, :], AF.Exp), reads=[R_chain], writes=[R_ex2])
        dve.op(lambda: V.tensor_copy(ex2[:, 4:8], xr_buf[:, :, 1023]), reads=xr_all, writes=[R_ex2])
        ag_exchange(ex2, ag2i[agi], ag2o[agi], cand2, R_ex2, R_cand2)
        sgu_u(0)
        sgu_u(1)
        R_p.inherit(R_vl4)
        dve.op(lambda: V.tensor_copy(pbuf[:, :, 0:2], hsel[:, 12:16].rearrange("p (c k) -> p c k", c=2)),
               reads=[R_hsel], writes=[R_p])
        for c2 in range(2):
            for sbk in range(2):
                bc_, bx_ = balloc(), balloc()
                proj(s_cg, c2, sbk, bc_)
                proj(s_xin, c2, sbk, bx_)
                t = tmp_next()
                act.op(lambda: A.activation(tmpA[t][:, :], ps[bc_][:, :], AF.Copy), reads=[R_bank[bc_]], writes=[R_tmp[t]])
                dve.op(lambda: V.tensor_tensor(pbuf[:, c2, 2 + sbk * 512:2 + (sbk + 1) * 512], tmpA[t][:, :], ps[bx_][:, :], ALU.mult),
                       reads=[R_tmp[t], R_bank[bx_]], writes=[R_p])
                bfree(bc_, bx_)
        ring_release(4)
        R_q.inherit([R_vt4[0]])
        sbg = ring_take()
        for sbk in range(2):
            for c2 in range(2):
                o = qbuf[:, c2, :]
                base = sbk * 512
                dve.op(lambda: V.tensor_scalar(o, pbuf[:, c2, base:base + 512], ppl[:, 32 + c2:33 + c2], None, ALU.mult),
                       reads=[R_p, R_par], writes=[R_q])
                for kk in (1, 2):
                    dve.op(lambda: V.scalar_tensor_tensor(o, pbuf[:, c2, base + kk:base + kk + 512],
                                                          ppl[:, 32 + 2 * kk + c2:33 + 2 * kk + c2], o, ALU.mult, ALU.add),
                           reads=[R_p, R_q, R_par], writes=[R_q])
            for c2 in range(2):
                b = balloc()
                proj(sbg, c2, sbk, b)
                dve.op(lambda: V.tensor_tensor(yT[:, 2 + c2, sbk * 512:(sbk + 1) * 512], qbuf[:, c2, :], ps[b][:, :], ALU.mult),
                       reads=[R_q, R_bank[b]], writes=[R_yT[2 + c2][sbk]])
                bfree(b)
        ring_release(1)
        for c in range(4):
            R_gg[c].w, R_gg[c].r = {}, {}
            R_gg[c].inherit(R_Tc if c < 2 else [R_xc[1], R_xcp[1]])
        sg = [ring_take(), ring_take()]
        for c in range(4):
            for sbk in range(2):
                b = balloc()
                proj(sg[c // 2], c % 2, sbk, b)
                act.op(lambda: A.activation(gg[c // 2][:, c % 2, sbk * 512:(sbk + 1) * 512], ps[b][:, :], GELU),
                       reads=[R_bank[b]], writes=[R_gg[c]])
                bfree(b)
        ring_release(2)
        prev = carry[l][:, :]
        dve.op(lambda: V.tensor_scalar(hin[:, :], prev, m[:, 0:1], None, ALU.mult), reads=[R_carry[l], R_par], writes=[R_chain])
        for q in range(4):
            sq = Sst[:, q, :]
            dve.op(lambda: V.tensor_tensor(sq, cand2[:, q, 0:4], prev, ALU.mult), reads=[R_cand2, R_chain, R_carry[l]], writes=[R_chain])
            dve.op(lambda: V.tensor_tensor(sq, sq, cand2[:, q, 4:8], ALU.add), reads=[R_cand2, R_chain], writes=[R_chain])
            if q < 3:
                dve.op(lambda: V.scalar_tensor_tensor(hin[:, :], sq, m[:, q + 1:q + 2], hin[:, :], ALU.mult, ALU.add),
                       reads=[R_chain, R_par], writes=[R_chain])
            prev = sq
        dve.op(lambda: V.tensor_copy(carry[l][:, :], Sst[:, 3, :]), reads=[R_chain], writes=[R_carry[l]])
        for c in range(4):
            dve.op(lambda: V.tensor_tensor_scan(xr_buf[:, c, 0:1024], a_buf[:, c, :], bx_buf[:, c, :], hin[:, c:c + 1], ALU.mult, ALU.add),
                   reads=R_a[c] + R_bx[c] + [R_chain], writes=xr_all)
        for sbk in range(2):
            for c in range(4):
                dve.op(lambda: V.tensor_tensor(yT[:, 4 + c, sbk * 512:(sbk + 1) * 512], gg[c // 2][:, c % 2, sbk * 512:(sbk + 1) * 512],
                                               xr_buf[:, c, sbk * 512:(sbk + 1) * 512], ALU.mult),
                       reads=[R_gg[c]] + xr_all, writes=[R_yT[4 + c][sbk]])
        for r in R_Tc:
            r.inherit(R_gg[0:2])
        R_xc[1].inherit(R_gg[2:4])
        R_xcp[1].inherit(R_gg[2:4])
        so = [ring_take() for _ in range(4)]
        for tt in range(NTT):
            bks = [balloc(), balloc()]
            for qd in range(4):
                o = ps[bks[qd // 2]][:, (qd % 2) * 256:(qd % 2 + 1) * 256]
                pe.group([(lambda kc=kc: P.matmul(o, lhsT=yT[:, kc, tt * 128:(tt + 1) * 128], rhs=ring_t[so[qd]][:, kc, :],
                                                  start=(kc == 0), stop=(kc == 7))) for kc in range(8)],
                         reads=[R_yT[kc][tt // 4] for kc in range(8)] + [R_ring[so[qd]]], writes=[R_bank[bks[qd // 2]]])
            ln_tile(tt, bks, step, False, pool_affine=True)
            bfree(*bks)
            if tt >= 2:
                transposes(tt - 2)
        deferred.extend([NTT - 2, NTT - 1])
        ring_release(4)

    for step in range(nsteps):
        for tt in range(NTT):
            if tt < NTT - 2:
                transposes(tt)
            else:
                deferred.append(tt)
        for l in range(nlayers):
            hs = ffn(l, 0, 0, step, False)
            mixer(l, step, hs)
            ffn(l, 1, 2, step, l == nlayers - 1)
    if dbg:
        if "xres" in dbg_d:
            sp.dma(DQ_OUT, [(dbg_d["xres"].rearrange("(tt p) d -> p tt d", p=128), xres[:, :, :])], reads=R_xres)
    sp.wait_tok(DQ_OUT, ctx.dcount[DQ_OUT])
    for tt in range(NTT):
        sp.wait_tok(DQ_OT0 + tt, ctx.dcount[DQ_OT0 + tt])
    for e in (pe, act, dve):
        sp.wait_tok(e.idx, e.count)
    es.close()
    return nc


def _pack_small(inp, nlayers=NLAYER):
    f = np.float32
    pp = np.zeros((2, 128, NPP), f)
    for l in range(nlayers):
        cw = np.asarray(inp["rg_conv_w"][l], f)
        pp[l, :, 0:16] = cw.reshape(4, 4, 128).transpose(2, 0, 1).reshape(128, 16)
        for off, key in ((16, "rg_conv_b"), (28, "rg_lambda")):
            pp[l, :, off:off + 4] = np.asarray(inp[key][l], f).reshape(4, 128).T
        for off, key in ((20, "rg_b_a"), (24, "rg_b_i")):
            pp[l, :, off:off + 4] = np.asarray(inp[key][l], f).reshape(4, 128).T
        sw = np.asarray(inp["sconv_w"][l], f)
        pp[l, :, 32:38] = sw.reshape(3, 2, 128).transpose(2, 0, 1).reshape(128, 6)
    wsT = np.ascontiguousarray(np.asarray(inp["sgu_w"], f).transpose(0, 3, 1, 2))
    blk = np.zeros((2, 128, 2, 4, 128), f)
    for l in range(nlayers):
        for gi, key in enumerate(("rg_w_a", "rg_w_i")):
            w = np.asarray(inp[key][l], f)
            for h in range(8):
                c, hh = h // 2, h % 2
                blk[l, hh * 64:(hh + 1) * 64, gi, c, hh * 64:(hh + 1) * 64] = w[h]
    sgub = np.ascontiguousarray(np.asarray(inp["sgu_b"], f).reshape(2, 2, 2, 128).transpose(0, 2, 1, 3))
    return pp, wsT, blk, sgub


def make_in_maps(inp, nsteps=NSTEP):
    f = np.float32
    pp, wsT, blk, sgub = _pack_small(inp)
    ident = np.eye(128, dtype=f)
    triu = np.triu(np.ones((128, 128), f))
    ind2 = np.zeros((2, 128), f)
    ind2[0, :64] = 1.0
    ind2[1, 64:] = 1.0
    shared = {k: np.ascontiguousarray(np.asarray(inp[k], f)) for k in
              ("ffn_w_gate", "ffn_w_up", "ffn_w_down", "w_in", "w_out", "ln_g", "ln_b", "sgu_ln_g", "sgu_ln_b")}
    shared.update(pp=pp, wsT=wsT, blk=blk, sgub=sgub, ident=ident, triu=triu, ind2=ind2)
    x = np.asarray(inp["x"], f)
    maps = []
    for core in range(8):
        b, r = core // 4, core % 4
        xs = np.stack([x[b, (4 * i + r) * NB:(4 * i + r + 1) * NB, :] for i in range(nsteps)])
        s = np.zeros((128, 4), f)
        s[:, r] = 1.0
        m = dict(shared)
        m["x"] = np.ascontiguousarray(xs)
        m["sel"] = s
        maps.append(m)
    return maps


_NC_CACHE = {}


def kernel(**inputs):
    if "nc" not in _NC_CACHE:
        _NC_CACHE["nc"] = build()
    nc = _NC_CACHE["nc"]
    maps = make_in_maps(inputs)
    res = run_bass_kernel_spmd(nc, maps, core_ids=list(range(8)))
    out = np.empty((2, NSTEP * 4 * NB, D), np.float32)
    for core in range(8):
        b, r = core // 4, core % 4
        o = res.results[core]["out"]
        for i in range(NSTEP):
            out[b, (4 * i + r) * NB:(4 * i + r + 1) * NB, :] = o[i]
    return out
```
